# Optimizing a Trainium2 kernel written in Bass

```python
import math
import jax, jax.numpy as jnp
from jax import lax
import numpy as np

D_MODEL = 1024
BATCH = 8
SEQ = 2048
DEPTH = 2

GRID_W = 64
CTX_LEN = 256

FOURIER_GROUPS = 4
FOURIER_GROUP_DIM = 64
FOURIER_WIDTH = FOURIER_GROUPS * FOURIER_GROUP_DIM
CONV_WIDTH = 256
CONV_K = 3
ATTN_HEADS = 4
QK_DIM = 64
V_DIM = 2 * QK_DIM
ATTN_QK_WIDTH = ATTN_HEADS * 2 * QK_DIM
ATTN_V_WIDTH = ATTN_HEADS * V_DIM
Q_BLOCK = 128
ROPE_BASE = 10000.0
ROPE_AXIS_DIM = QK_DIM // 2
N_BRANCH = 3
D_FF = -(-8 * D_MODEL // (3 * 256)) * 256
EPS = 1e-6

F_OFF = 0
CB_OFF = F_OFF + FOURIER_WIDTH
CC_OFF = CB_OFF + CONV_WIDTH
CX_OFF = CC_OFF + CONV_WIDTH
Q_OFF = CX_OFF + CONV_WIDTH
K_OFF = Q_OFF + ATTN_QK_WIDTH
V_OFF = K_OFF + ATTN_QK_WIDTH
G_OFF = V_OFF + ATTN_V_WIDTH
D_IN = G_OFF + N_BRANCH * D_MODEL

kernel_name = "hybrid_fourier_conv_diffattn_dit_block"


def rmsnorm(x, g):
    xf = x.astype(jnp.float32)
    y = xf * lax.rsqrt(jnp.mean(xf * xf, axis=-1, keepdims=True) + EPS)
    return (y * g.astype(jnp.float32)).astype(x.dtype)


def modulate(h, shift, scale):
    return h * (1 + scale[:, None, :]) + shift[:, None, :]


def rot_half(x, cos, sin):
    n = x.shape[-1] // 2
    x1, x2 = x[..., :n], x[..., n:]
    return jnp.concatenate([x1 * cos - x2 * sin, x2 * cos + x1 * sin], axis=-1)


def axial_rope(x, cos_r, sin_r, cos_c, sin_c):
    a = ROPE_AXIS_DIM
    return jnp.concatenate([rot_half(x[..., :a], cos_r, sin_r),
                            rot_half(x[..., a:], cos_c, sin_c)], axis=-1)


def rope_tables(pos, dtype):
    freqs = ROPE_BASE ** (-jnp.arange(0, ROPE_AXIS_DIM, 2, dtype=jnp.float32) / ROPE_AXIS_DIM)
    ang = pos.astype(jnp.float32)[:, None] * freqs[None, :]
    ang = ang[:, None, None, :]
    return jnp.cos(ang).astype(dtype), jnp.sin(ang).astype(dtype)


def split_qkv(p):
    B, L, _ = p.shape
    q = p[..., Q_OFF:K_OFF].reshape(B, L, ATTN_HEADS, 2, QK_DIM)
    k = p[..., K_OFF:V_OFF].reshape(B, L, ATTN_HEADS, 2, QK_DIM)
    v = p[..., V_OFF:G_OFF].reshape(B, L, ATTN_HEADS, V_DIM)
    return q, k, v


def to_heads_qk(t):
    return t.transpose(0, 2, 3, 1, 4)


def diff_attn_core(q, k, v, lam):
    s = jnp.einsum('bhcqd,bhckd->bhcqk', q, k).astype(jnp.float32) * (QK_DIM ** -0.5)
    pr = jax.nn.softmax(s, axis=-1)
    a = pr[:, :, 0] - lam * pr[:, :, 1]
    return jnp.einsum('bhqk,bhkv->bhqv', a.astype(v.dtype), v)


def diff_attn_blocked(q, k, v, lam):
    B, H, _, L, d = q.shape
    nb = L // Q_BLOCK
    qb = jnp.moveaxis(q.reshape(B, H, 2, nb, Q_BLOCK, d), 3, 0)
    ob = lax.map(lambda qi: diff_attn_core(qi, k, v, lam), qb)
    return jnp.moveaxis(ob, 0, 2).reshape(B, H, L, V_DIM)


def fourier_mix(u):
    B, L, _ = u.shape
    ug = u.astype(jnp.float32).reshape(B, L, FOURIER_GROUPS, FOURIER_GROUP_DIM)
    y = jnp.fft.fft2(ug, axes=(1, 3), norm='ortho').real
    return y.reshape(B, L, FOURIER_WIDTH).astype(u.dtype)


def short_gated_conv(bg, cg, xin, w, b):
    z = cg * xin
    zp = jnp.pad(z, ((0, 0), (1, 1), (0, 0)))
    conv = zp[:, :-2] * w[0] + zp[:, 1:-1] * w[1] + zp[:, 2:] * w[2] + b
    return bg * conv


def merge_branches(p, attn_o, lambda_init, conv_w, conv_b, w_fourier_out, w_conv_out,
                   w_attn_out, subln_g, w_out):
    B, L, _ = p.shape
    y_f = fourier_mix(p[..., F_OFF:CB_OFF]) @ w_fourier_out
    y_c = short_gated_conv(p[..., CB_OFF:CC_OFF], p[..., CC_OFF:CX_OFF], p[..., CX_OFF:Q_OFF],
                           conv_w, conv_b) @ w_conv_out
    o = rmsnorm(attn_o.transpose(0, 2, 1, 3), subln_g) * (1.0 - lambda_init)
    y_a = o.reshape(B, L, ATTN_V_WIDTH) @ w_attn_out
    gates = jax.nn.sigmoid(p[..., G_OFF:].reshape(B, L, N_BRANCH, D_MODEL))
    y = gates[:, :, 0] * y_f + gates[:, :, 1] * y_c + gates[:, :, 2] * y_a
    return y @ w_out


def swiglu(h, wg, wu, wd):
    return (jax.nn.silu(h @ wg) * (h @ wu)) @ wd


def setup_inputs(seed: int = 0) -> dict:
    key = jax.random.key(seed)
    ks = jax.random.split(key, 25)
    n = lambda k, shape, s: jax.random.normal(k, shape, jnp.float32) * s
    D = D_MODEL
    return {
        "x": n(ks[0], (BATCH, SEQ, D), 1.0),
        "c": n(ks[1], (BATCH, D), 1.0),
        "ctx": n(ks[2], (BATCH, CTX_LEN, D), 1.0),
        "c_ctx": n(ks[3], (D,), 1.0),
        "w_ada": n(ks[4], (DEPTH, D, 6 * D), 0.2 * D ** -0.5),
        "b_ada": n(ks[5], (DEPTH, 6 * D), 0.01),
        "norm1_g": 1.0 + n(ks[6], (DEPTH, D), 0.02),
        "norm2_g": 1.0 + n(ks[7], (DEPTH, D), 0.02),
        "w_in": n(ks[8], (DEPTH, D, D_IN), D ** -0.5),
        "b_in": n(ks[9], (DEPTH, D_IN), 0.01),
        "conv_w": n(ks[10], (DEPTH, CONV_K, CONV_WIDTH), CONV_K ** -0.5),
        "conv_b": n(ks[11], (DEPTH, CONV_WIDTH), 0.01),
        "w_fourier_out": n(ks[12], (DEPTH, FOURIER_WIDTH, D), FOURIER_WIDTH ** -0.5),
        "w_conv_out": n(ks[13], (DEPTH, CONV_WIDTH, D), CONV_WIDTH ** -0.5),
        "w_attn_out": n(ks[14], (DEPTH, ATTN_V_WIDTH, D), ATTN_V_WIDTH ** -0.5),
        "lambda_q1": n(ks[15], (DEPTH, QK_DIM), 0.1),
        "lambda_k1": n(ks[16], (DEPTH, QK_DIM), 0.1),
        "lambda_q2": n(ks[17], (DEPTH, QK_DIM), 0.1),
        "lambda_k2": n(ks[18], (DEPTH, QK_DIM), 0.1),
        "subln_g": 1.0 + n(ks[19], (DEPTH, V_DIM), 0.02),
        "w_out": n(ks[20], (DEPTH, D, D), D ** -0.5),
        "w_ffn_gate": n(ks[21], (DEPTH, D, D_FF), D ** -0.5),
        "w_ffn_up": n(ks[22], (DEPTH, D, D_FF), D ** -0.5),
        "w_ffn_down": n(ks[23], (DEPTH, D_FF, D), D_FF ** -0.5),
        "final_g": 1.0 + n(ks[24], (D,), 0.02),
    }


def reference(x, c, ctx, c_ctx, w_ada, b_ada, norm1_g, norm2_g, w_in, b_in, conv_w, conv_b,
              w_fourier_out, w_conv_out, w_attn_out, lambda_q1, lambda_k1, lambda_q2, lambda_k2,
              subln_g, w_out, w_ffn_gate, w_ffn_up, w_ffn_down, final_g):
    L = x.shape[1]
    n_rows = L // GRID_W
    row = jnp.repeat(jnp.arange(n_rows), GRID_W)
    col = jnp.tile(jnp.arange(GRID_W), n_rows)
    cos_r, sin_r = rope_tables(row, x.dtype)
    cos_c, sin_c = rope_tables(col, x.dtype)

    s_c = jax.nn.silu(c)
    s_cc = jax.nn.silu(c_ctx)[None, :]
    xl, xc = x, ctx
    for l in range(DEPTH):
        last = l == DEPTH - 1
        lambda_init = 0.8 - 0.6 * math.exp(-0.3 * l)
        mod = s_c @ w_ada[l] + b_ada[l]
        modc = s_cc @ w_ada[l] + b_ada[l]
        sh1, sc1, g1, sh2, sc2, g2 = jnp.split(mod, 6, axis=-1)
        csh1, csc1, cg1, csh2, csc2, cg2 = jnp.split(modc, 6, axis=-1)

        h = modulate(rmsnorm(xl, norm1_g[l]), sh1, sc1)
        hc = modulate(rmsnorm(xc, norm1_g[l]), csh1, csc1)
        p = h @ w_in[l] + b_in[l]
        pc = hc @ w_in[l] + b_in[l]

        q, k, v = split_qkv(p)
        q = axial_rope(q, cos_r, sin_r, cos_c, sin_c)
        k = axial_rope(k, cos_r, sin_r, cos_c, sin_c)
        qc, kc, vc = split_qkv(pc)
        q, k, qc, kc = to_heads_qk(q), to_heads_qk(k), to_heads_qk(qc), to_heads_qk(kc)
        v, vc = v.transpose(0, 2, 1, 3), vc.transpose(0, 2, 1, 3)

        lam = (jnp.exp(jnp.sum(lambda_q1[l].astype(jnp.float32) * lambda_k1[l].astype(jnp.float32)))
               - jnp.exp(jnp.sum(lambda_q2[l].astype(jnp.float32) * lambda_k2[l].astype(jnp.float32)))
               + lambda_init)

        k_all = jnp.concatenate([k, kc], axis=3)
        v_all = jnp.concatenate([v, vc], axis=2)
        o = diff_attn_blocked(q, k_all, v_all, lam)

        xl = xl + g1[:, None, :] * merge_branches(p, o, lambda_init, conv_w[l], conv_b[l],
                                                  w_fourier_out[l], w_conv_out[l], w_attn_out[l],
                                                  subln_g[l], w_out[l])
        h2 = modulate(rmsnorm(xl, norm2_g[l]), sh2, sc2)
        xl = xl + g2[:, None, :] * swiglu(h2, w_ffn_gate[l], w_ffn_up[l], w_ffn_down[l])

        if not last:
            oc = diff_attn_core(qc, kc, vc, lam)
            xc = xc + cg1[:, None, :] * merge_branches(pc, oc, lambda_init, conv_w[l], conv_b[l],
                                                       w_fourier_out[l], w_conv_out[l],
                                                       w_attn_out[l], subln_g[l], w_out[l])
            hc2 = modulate(rmsnorm(xc, norm2_g[l]), csh2, csc2)
            xc = xc + cg2[:, None, :] * swiglu(hc2, w_ffn_gate[l], w_ffn_up[l], w_ffn_down[l])

    return rmsnorm(xl, final_g)
```

```python
import math
import contextlib
import numpy as np
import ml_dtypes
import concourse.bass as bass
import concourse.mybir as mybir
from concourse.bass_utils import run_bass_kernel_spmd

F32 = mybir.dt.float32
BF16 = mybir.dt.bfloat16
AF = mybir.ActivationFunctionType
ALU = mybir.AluOpType

D = 1024
L = 2048
CTX = 256
NTOK = L + CTX
DEPTH = 2
DFF = 2816
F_OFF, CB_OFF, CC_OFF, CX_OFF, Q_OFF, K_OFF, V_OFF, G_OFF = 0, 256, 512, 768, 1024, 1536, 2048, 2560
EPS = 1e-6
TB = [(0, 512), (512, 512), (1024, 512), (1536, 512), (2048, 256)]
NV_L = 125
ENGS = ("pe", "act", "dve", "pool", "sp")


class Tile:
    __slots__ = ("name", "lw", "rd", "dma_sem", "dma_cnt", "hist")

    def __init__(self, name):
        self.name = name
        self.lw = None
        self.rd = []
        self.dma_sem = None
        self.dma_cnt = 0
        self.hist = []


class Prog:
    def __init__(self, nc):
        self.nc = nc
        self.ops = {e: [] for e in ENGS}
        self.n_dma_sem = 0

    def _deps(self, eng, reads, writes):
        deps = []
        for t in reads:
            if t.lw is not None:
                deps.append(t.lw)
        for t in writes:
            if t.lw is not None:
                deps.append(t.lw)
            deps.extend(t.rd)
        out = []
        seen = set()
        for d in deps:
            if d in seen:
                continue
            seen.add(d)
            if d[0] == "eng" and d[1] == eng and eng == "pe":
                continue
            out.append(d)
        return out

    def op(self, eng, fn, reads=(), writes=()):
        deps = self._deps(eng, reads, writes)
        ev = ("eng", eng, len(self.ops[eng]))
        self.ops[eng].append({"fn": fn, "deps": deps, "ev": ev, "kind": "c", "lazy": ()})
        for t in reads:
            t.rd.append(ev)
            t.hist.append(ev)
        for t in writes:
            t.lw = ev
            t.rd = []
            t.hist.append(ev)
        return ev

    def dma(self, queue, fn, reads=(), writes=(), sem_tile=None, lazy=()):
        deps = self._deps(queue, reads, writes)
        st = sem_tile or (writes[0] if writes else reads[0])
        if st.dma_sem is None:
            st.dma_sem = self.n_dma_sem
            self.n_dma_sem += 1
        st.dma_cnt += 16
        ev = ("dma", st.dma_sem, st.dma_cnt)
        self.ops[queue].append({"fn": fn, "deps": deps, "ev": ev, "kind": "d", "lazy": tuple(lazy)})
        for t in reads:
            t.rd.append(ev)
            t.hist.append(ev)
        for t in writes:
            t.lw = ev
            t.rd = []
            t.hist.append(ev)
        return ev

    def alias(self, new_tiles, old_tiles):
        ev = []
        for t in old_tiles:
            if t.lw is not None:
                ev.append(t.lw)
            ev.extend(t.rd)
        ev = list(dict.fromkeys(ev))
        for t in new_tiles:
            t.rd = list(dict.fromkeys(list(t.rd) + ev))

    def final_wait(self, queue, tiles):
        deps = []
        for t in tiles:
            if t.lw is not None:
                deps.append(t.lw)
            deps.extend(t.rd)
        self.ops[queue].append({"fn": None, "deps": list(dict.fromkeys(deps)), "ev": None, "kind": "w", "lazy": ()})

    def emit(self):
        nc = self.nc
        need = {e: set() for e in ENGS}
        for e in ENGS:
            for o in self.ops[e]:
                if o["lazy"]:
                    extra = [ev for t in o["lazy"] for ev in t.hist if ev != o["ev"]]
                    o["deps"] = list(dict.fromkeys(list(o["deps"]) + extra))
                for d in o["deps"]:
                    if d[0] == "eng":
                        need[d[1]].add(d[2])
        rank = {}
        for e in ENGS:
            r = 0
            for i, o in enumerate(self.ops[e]):
                if o["kind"] == "c" and i in need[e]:
                    r += 1
                    rank[(e, i)] = r
        with contextlib.ExitStack() as st:
            esem = {e: st.enter_context(nc.semaphore("s_" + e)) for e in ENGS}
            dsem = [st.enter_context(nc.semaphore("d%d" % i)) for i in range(self.n_dma_sem)]
            block = st.enter_context(nc.Block())

            def run(e, engobj):
                waited = {}
                for i, o in enumerate(self.ops[e]):
                    for d in o["deps"]:
                        if d[0] == "eng":
                            key, val, sem = ("e", d[1]), rank[(d[1], d[2])], esem[d[1]]
                        else:
                            key, val, sem = ("d", d[1]), d[2], dsem[d[1]]
                        if waited.get(key, 0) >= val:
                            continue
                        waited[key] = val
                        engobj.wait_ge(sem, val)
                    if o["fn"] is None:
                        continue
                    ins = o["fn"](engobj)
                    if o["kind"] == "d":
                        ins.then_inc(dsem[o["ev"][1]], 16)
                    elif (e, i) in rank:
                        ins.then_inc(esem[e], 1)

            block.tensor(lambda eng: run("pe", eng))
            block.scalar(lambda eng: run("act", eng))
            block.vector(lambda eng: run("dve", eng))
            block.gpsimd(lambda eng: run("pool", eng))
            block.sync(lambda eng: run("sp", eng))


def mk(name, *args, **kw):
    return lambda e: getattr(e, name)(*args, **kw)


def _tile_kxn(W):
    k, n = W.shape
    return np.ascontiguousarray(W.reshape(k // 128, 128, n).transpose(1, 0, 2)).reshape(128, (k // 128) * n)


def _rot_perm(cols):
    cols = np.asarray(cols)
    j = np.arange(cols.shape[0])
    return cols[(j // 32) * 32 + ((j % 32) + 16) % 32]


def _weight_plan():
    plan = []
    for h in range(4):
        plan.append(("attn_qk", h, 8 * 256))
        plan.append(("attn_v", h, 8 * 128))
    plan.append(("four", 0, 8 * 256))
    plan.append(("conv_cx", 0, 8 * 512))
    plan.append(("conv_b", 0, 8 * 256))
    for g in range(4):
        plan.append(("gate_fc", g, 8 * 512))
        plan.append(("gate_a_wo", g, 8 * 512))
        plan.append(("wout", g, 2 * 1024))
    for fg in range(6):
        nf = 4 if fg < 5 else 2
        plan.append(("ffn_g", fg, 8 * nf * 128))
        plan.append(("ffn_u", fg, 8 * nf * 128))
        plan.append(("ffn_d", fg, nf * 1024))
    return plan


def _pack_weights(inp, l):
    w_in = inp["w_in"][l]
    wcat = np.concatenate([inp["w_fourier_out"][l], inp["w_conv_out"][l], inp["w_attn_out"][l]], axis=0)
    parts = []
    for kind, a, free in _weight_plan():
        if kind == "attn_qk":
            q = Q_OFF + a * 128 + np.arange(128)
            k = K_OFF + a * 128 + np.arange(128)
            cols = np.concatenate([q, k])
            t = _tile_kxn(w_in[:, cols])
        elif kind == "attn_v":
            t = _tile_kxn(w_in[:, V_OFF + a * 128:V_OFF + (a + 1) * 128])
        elif kind == "four":
            t = _tile_kxn(w_in[:, F_OFF:F_OFF + 256])
        elif kind == "conv_cx":
            cols = np.concatenate([CC_OFF + np.arange(128), CX_OFF + np.arange(128),
                                   CC_OFF + 128 + np.arange(128), CX_OFF + 128 + np.arange(128)])
            t = _tile_kxn(w_in[:, cols])
        elif kind == "conv_b":
            t = _tile_kxn(w_in[:, CB_OFF:CB_OFF + 256])
        elif kind == "gate_fc":
            cols = np.concatenate([G_OFF + a * 256 + np.arange(256), G_OFF + 1024 + a * 256 + np.arange(256)])
            t = _tile_kxn(w_in[:, cols])
        elif kind == "gate_a_wo":
            t = np.concatenate([_tile_kxn(w_in[:, G_OFF + 2048 + a * 256:G_OFF + 2048 + (a + 1) * 256]),
                                _tile_kxn(wcat[:, a * 256:(a + 1) * 256])], axis=1)
        elif kind == "wout":
            t = _tile_kxn(inp["w_out"][l][a * 256:(a + 1) * 256, :])
        elif kind in ("ffn_g", "ffn_u"):
            nf = 4 if a < 5 else 2
            W = inp["w_ffn_gate" if kind == "ffn_g" else "w_ffn_up"][l]
            t = _tile_kxn(W[:, a * 512:a * 512 + nf * 128])
        elif kind == "ffn_d":
            nf = 4 if a < 5 else 2
            t = _tile_kxn(inp["w_ffn_down"][l][a * 512:a * 512 + nf * 128, :])
        assert t.shape == (128, free), (kind, t.shape, free)
        parts.append(t)
    return np.ascontiguousarray(np.concatenate(parts, axis=1), dtype=np.float32)


def _col(v):
    v = np.asarray(v)
    return np.ascontiguousarray(v.reshape(-1, 128).T)


def _pack_vecs(inp):
    cols = []
    for l in range(DEPTH):
        b_in = inp["b_in"][l]
        c = [_col(inp["b_ada"][l]), _col(inp["norm1_g"][l]), _col(inp["norm2_g"][l])]
        for h in range(4):
            q = Q_OFF + h * 128 + np.arange(128)
            k = K_OFF + h * 128 + np.arange(128)
            for cc in (q, _rot_perm(q), k, _rot_perm(k)):
                c.append(_col(b_in[cc]))
        c.append(_col(b_in[F_OFF:F_OFF + 256]))
        for cc in (CC_OFF, CX_OFF, CC_OFF + 128, CX_OFF + 128):
            c.append(_col(b_in[cc:cc + 128]))
        c.append(_col(b_in[CB_OFF:CB_OFF + 256]))
        c.append(_col(b_in[G_OFF:G_OFF + 3072]))
        for k in range(3):
            c.append(_col(inp["conv_w"][l][k]))
        c.append(_col(inp["conv_b"][l]))
        c.append(_col(inp["subln_g"][l]))
        lam = np.zeros((128, 4), np.float32)
        for i, nm in enumerate(("lambda_q1", "lambda_k1", "lambda_q2", "lambda_k2")):
            lam[:64, i] = inp[nm][l]
        c.append(lam)
        blk = np.concatenate(c, axis=1)
        assert blk.shape == (128, NV_L), blk.shape
        cols.append(blk)
    cols.append(_col(inp["final_g"]))
    return np.ascontiguousarray(np.concatenate(cols, axis=1), dtype=np.float32)


_CONST_CACHE = {}


def _const_tables():
    if _CONST_CACHE:
        return _CONST_CACHE
    t = np.arange(L)
    pos_r, pos_c = t // 64, t % 64
    freqs = 10000.0 ** (-(np.arange(0, 32, 2, dtype=np.float64)) / 32.0)
    rope = np.zeros((128, 2, L), np.float64)
    for p in range(128):
        j = p % 64
        pos = pos_r if j < 32 else pos_c
        i = j % 16
        half = (j % 32) // 16
        ang = pos.astype(np.float32).astype(np.float64) * np.float32(freqs[i]).astype(np.float64)
        rope[p, 0] = np.cos(ang)
        rope[p, 1] = np.sin(ang) * (-1.0 if half == 0 else 1.0)
    _CONST_CACHE["rope"] = rope.astype(np.float32)
    tt = np.arange(L, dtype=np.float64)
    ang = 2.0 * np.pi * ((tt[:, None] * tt[None, :]) % L) / L
    CL = np.cos(ang) / math.sqrt(L)
    SL = -np.sin(ang) / math.sqrt(L)
    tab = np.stack([CL, SL], axis=0)
    tab = tab.reshape(2, 8, 2, 128, 4, 512)
    tab = tab.transpose(4, 1, 3, 0, 2, 5)
    _CONST_CACHE["dftL"] = np.ascontiguousarray(tab).reshape(4, 8, 128, 2048).astype(ml_dtypes.bfloat16)
    tc = np.arange(CTX, dtype=np.float64)
    angc = 2.0 * np.pi * ((tc[:, None] * tc[None, :]) % CTX) / CTX
    tabc = np.stack([np.cos(angc), -np.sin(angc)], axis=0) / math.sqrt(CTX)
    tabc = tabc.reshape(2, 2, 128, CTX).transpose(2, 0, 1, 3)
    _CONST_CACHE["dftC"] = np.ascontiguousarray(tabc).reshape(128, 1024).astype(ml_dtypes.bfloat16)
    c = np.arange(128)
    same = (c[:, None] // 64) == (c[None, :] // 64)
    angh = 2.0 * np.pi * (((c[:, None] % 64) * (c[None, :] % 64)) % 64) / 64.0
    ch = np.concatenate([np.where(same, np.cos(angh), 0.0), np.where(same, np.sin(angh), 0.0)], axis=1) / 8.0
    _CONST_CACHE["dftch"] = np.ascontiguousarray(ch).astype(ml_dtypes.bfloat16)
    pm = np.zeros((128, 128), np.float32)
    pm[_rot_perm(np.arange(128)), np.arange(128)] = 1.0
    _CONST_CACHE["permm"] = pm.astype(ml_dtypes.bfloat16)
    return _CONST_CACHE


def build_program(debug=None, stop_after=None, n_layers=DEPTH):
    debug = debug or []
    nc = bass.Bass("TRN2", target_bir_lowering=False)
    plan = _weight_plan()
    TOT = sum(f for _, _, f in plan)
    xt_d = nc.dram_tensor("xt", [D, NTOK], F32, kind="ExternalInput").ap()
    cv_d = nc.dram_tensor("cv", [128, 16], F32, kind="ExternalInput").ap()
    wada_d = nc.dram_tensor("wada", [DEPTH, 48, 128, 1024], F32, kind="ExternalInput").ap()
    wb_d = nc.dram_tensor("wb", [DEPTH, 128, TOT], F32, kind="ExternalInput").ap()
    vecs_d = nc.dram_tensor("vecs", [128, DEPTH * NV_L + 8], F32, kind="ExternalInput").ap()
    bvb_d = nc.dram_tensor("bvb", [DEPTH, 128, 512], F32, kind="ExternalInput").ap()
    rope_d = nc.dram_tensor("rope", [128, 2 * L], F32, kind="ExternalInput").ap()
    dftL_d = nc.dram_tensor("dftL", [4, 8, 128, 2048], BF16, kind="ExternalInput").ap()
    dftC_d = nc.dram_tensor("dftC", [128, 1024], BF16, kind="ExternalInput").ap()
    dftch_d = nc.dram_tensor("dftch", [128, 256], BF16, kind="ExternalInput").ap()
    permm_d = nc.dram_tensor("permm", [128, 128], BF16, kind="ExternalInput").ap()
    out_d = nc.dram_tensor("outT", [D, L], F32, kind="ExternalOutput").ap()
    dbg_out = {}

    P = Prog(nc)
    es = contextlib.ExitStack()
    with es:
        ARENA_F32 = 53200
        arena = es.enter_context(nc.sbuf_tensor("arena", [128, ARENA_F32], F32))
        psum_all = es.enter_context(nc.psum_tensor("psall", [128, 4096], F32))
        PS = [Tile("ps%d" % i) for i in range(8)]
        ps = [psum_all[:, i * 512:(i + 1) * 512] for i in range(8)]

        def view(off, nelem, dt):
            if dt == F32:
                assert off % 4 == 0
                return arena[:, off // 4: off // 4 + nelem]
            assert off % 4 == 0 and nelem % 2 == 0
            return arena[:, off // 4: off // 4 + nelem // 2].bitcast(BF16)

        O_X = 0
        O_H = O_X + 8 * NTOK * 4
        O_BR = O_H + 8 * NTOK * 2
        O_RING = O_BR + 8 * NTOK * 2
        O_CONST = O_RING + 3 * 8192
        O_PH = O_CONST + 8192
        PH_SIZE = ARENA_F32 * 4 - O_PH
        xT = view(O_X, 8 * NTOK, F32).rearrange("p (k t) -> p k t", k=8)
        hT = view(O_H, 8 * NTOK, BF16).rearrange("p (k t) -> p k t", k=8)
        oT = view(O_BR, 4 * NTOK, BF16).rearrange("p (k t) -> p k t", k=4)
        yfT = view(O_BR + 4 * NTOK * 2, 2 * NTOK, BF16).rearrange("p (k t) -> p k t", k=2)
        ycT = view(O_BR + 6 * NTOK * 2, 2 * NTOK, BF16).rearrange("p (k t) -> p k t", k=2)
        ropeT = view(O_BR + 4 * NTOK * 2, 2 * L, F32).rearrange("p (k t) -> p k t", k=2)
        ring = [view(O_RING + i * 8192, 4096, BF16) for i in range(3)]
        T_x = [[Tile("x%d_%d" % (k, b)) for b in range(5)] for k in range(8)]
        T_h = [[Tile("h%d_%d" % (k, b)) for b in range(5)] for k in range(8)]
        T_o = [[Tile("o%d_%d" % (k, b)) for b in range(5)] for k in range(4)]
        T_yf = [[Tile("yf%d_%d" % (k, b)) for b in range(5)] for k in range(2)]
        T_yc = [[Tile("yc%d_%d" % (k, b)) for b in range(5)] for k in range(2)]
        T_rope = Tile("rope")
        T_ring = [Tile("ring%d" % i) for i in range(3)]

        co = [O_CONST]

        def calloc(nbytes):
            o = co[0]
            co[0] += (nbytes + 3) // 4 * 4
            assert co[0] <= O_PH
            return o
        NV = DEPTH * NV_L + 8
        vecs = view(calloc(NV * 4), NV, F32)
        modv = view(calloc(DEPTH * 96 * 4), DEPTH * 96, F32).rearrange("p (l c m) -> p l c m", l=DEPTH, m=2)
        derived = view(calloc(DEPTH * 64 * 4), DEPTH * 64, F32).rearrange("p (l j k m) -> p l j k m", l=DEPTH, j=4, m=2)
        sv = view(calloc(64), 16, F32).rearrange("p (k m) -> p k m", m=2)
        cvs = view(calloc(64), 16, F32)
        svb = view(calloc(32), 16, BF16).rearrange("p (k m) -> p k m", m=2)
        lamv = view(calloc(DEPTH * 16), DEPTH * 4, F32).rearrange("p (l k) -> p l k", l=DEPTH)
        gsub = view(calloc(DEPTH * 4), DEPTH, F32)
        ones_bf = view(calloc(256), 128, BF16)
        ones_f = view(calloc(512), 128, F32)
        dftch = view(calloc(512), 256, BF16)
        dftC = view(calloc(2048), 1024, BF16).rearrange("p (c t n) -> p c t n", c=2, t=2)
        epsv = view(calloc(4), 1, F32)
        permM = view(calloc(256), 128, BF16)
        T_const = Tile("const")
        T_modv = Tile("modv")

        def vcol(l, j):
            return vecs[:, l * NV_L + j: l * NV_L + j + 1]

        def tap(name, ap, tiles, shape, dt):
            if name not in debug:
                return
            d = nc.dram_tensor("dbg_" + name, shape, dt, kind="ExternalOutput").ap()
            dbg_out[name] = Tile("dbg_" + name)
            P.dma("sp", mk("dma_start", out=d, in_=ap), reads=tiles, writes=[dbg_out[name]])

        wseq = []
        for l in range(n_layers):
            off = 0
            for kind, a, free in plan:
                wseq.append((l, off, free, kind, a))
                off += free
        wstate = {"used": 0, "issued": 0}
        T_wgen = [Tile("wgen%d" % i) for i in range(len(wseq))]

        def w_issue_upto(n):
            while wstate["issued"] < min(n, len(wseq)):
                i = wstate["issued"]
                l_, off_, free_, _, _ = wseq[i]
                s_ = i % 3
                P.dma("pool", mk("dma_start",
                    out=ring[s_][:, 0:free_], in_=wb_d[l_, :, off_:off_ + free_], max_dma_last_dim=8192),
                    writes=[T_wgen[i]], sem_tile=T_ring[s_], lazy=([T_wgen[i - 3]] if i >= 3 else []))
                wstate["issued"] += 1

        def w_next(kind, a):
            i = wstate["used"]
            l, off, free, k2, a2 = wseq[i]
            assert (k2, a2) == (kind, a), (kind, a, k2, a2)
            w_issue_upto(i + 3)
            wstate["used"] += 1
            return ring[i % 3], T_wgen[i]

        ph = {"off": 0, "tiles": [], "prev": []}

        def ph_reset():
            ph["prev"] = ph["prev"] + ph["tiles"]
            ph["tiles"] = []
            ph["off"] = 0

        def ph_alloc(name, nelem, dt):
            nb = nelem * (4 if dt == F32 else 2)
            nb = (nb + 3) // 4 * 4
            assert ph["off"] + nb <= PH_SIZE, ("phase region overflow", name, ph["off"], nb)
            ap = view(O_PH + ph["off"], nelem, dt)
            ph["off"] += nb
            t = Tile(name)
            P.alias([t], ph["prev"])
            ph["tiles"].append(t)
            return ap, t

        def ph_tiles(tl):
            P.alias(tl, ph["prev"])
            ph["tiles"].extend(tl)

        def ph_commit():
            ph["prev"] = []

        P.dma("sp", mk("dma_start", out=vecs, in_=vecs_d), writes=[T_const])
        P.dma("sp", mk("dma_start", out=cvs, in_=cv_d), writes=[T_const])
        P.dma("sp", mk("dma_start", out=dftch, in_=dftch_d), writes=[T_const])
        P.dma("sp", mk("dma_start", out=permM, in_=permm_d), writes=[T_const])
        P.dma("sp", mk("dma_start", out=dftC.rearrange("p c t n -> p (c t n)"), in_=dftC_d), writes=[T_const])
        P.op("dve", mk("memset", ones_bf, 1.0), writes=[T_const])
        P.op("dve", mk("memset", ones_f, 1.0), writes=[T_const])
        P.op("dve", mk("memset", epsv, EPS), writes=[T_const])
        for b, (t0, tn) in enumerate(TB):
            P.dma("sp", mk("dma_start",
                out=xT[:, :, t0:t0 + tn], in_=xt_d.rearrange("(k p) t -> p k t", p=128)[:, :, t0:t0 + tn]),
                writes=[T_x[k][b] for k in range(8)], sem_tile=T_x[0][b])
        P.op("act", mk("activation", out=svb.rearrange("p k m -> p (k m)"), in_=cvs, func=AF.Silu),
             reads=[T_const], writes=[T_modv])
        T_mod = [Tile("mod%d" % l) for l in range(DEPTH)]

        def adaln_cols(l, cols, wst, cnt, bank, tile=None):
            for col in cols:
                wap, wt = wst[cnt[0] % len(wst)]
                cnt[0] += 1
                P.dma("pool", mk("dma_start", out=wap, in_=wada_d[l, col], max_dma_last_dim=8192), writes=[wt])
                w3 = wap.rearrange("p (k n) -> p k n", k=8)
                for kc in range(8):
                    P.op("pe", mk("matmul", out=ps[bank][:, 0:2], lhsT=w3[:, kc, :], rhs=svb[:, kc, :],
                                  start=(kc == 0), stop=(kc == 7)), reads=[wt, T_modv], writes=[PS[bank]])
                P.op("dve", mk("tensor_scalar", out=modv[:, l, col, :], in0=ps[bank][:, 0:2],
                               scalar1=vecs[:, l * NV_L + col: l * NV_L + col + 1], scalar2=None, op0=ALU.add),
                     reads=[PS[bank], T_const], writes=[tile or T_mod[l]])

        ada_state = {"next": 0}

        def ada_begin(l, slots):
            pairs = []
            for (wap, wt) in slots:
                col = ada_state["next"]
                if col >= 48:
                    break
                ada_state["next"] += 1
                P.dma("pool", mk("dma_start", out=wap, in_=wada_d[l, col], max_dma_last_dim=8192), writes=[wt])
                pairs.append((col, wap, wt))
            return pairs

        def ada_end(l, pairs, bank):
            for (col, wap, wt) in pairs:
                w3 = wap.rearrange("p (k n) -> p k n", k=8)
                for kc in range(8):
                    P.op("pe", mk("matmul", out=ps[bank][:, 0:2], lhsT=w3[:, kc, :], rhs=svb[:, kc, :],
                                  start=(kc == 0), stop=(kc == 7)), reads=[wt, T_modv], writes=[PS[bank]])
                P.op("dve", mk("tensor_scalar", out=modv[:, l, col, :], in0=ps[bank][:, 0:2],
                               scalar1=vecs[:, l * NV_L + col: l * NV_L + col + 1], scalar2=None, op0=ALU.add),
                     reads=[PS[bank], T_const], writes=[T_mod[l]])

        def adaln_finish_a(l, tile):
            for m in range(2):
                P.op("dve", mk("scalar_tensor_tensor",
                    out=derived[:, l, 0, :, m], in0=modv[:, l, 8:16, m], scalar=1.0,
                    in1=vecs[:, l * NV_L + 48: l * NV_L + 56], op0=ALU.add, op1=ALU.mult),
                    reads=[tile, T_const], writes=[tile])

        def adaln_finish(l):
            adaln_finish_a(l, T_mod[l])
            adaln_finish_b(l)

        def adaln_finish_b(l):
            for m in range(2):
                P.op("dve", mk("scalar_tensor_tensor",
                    out=derived[:, l, 1, :, m], in0=modv[:, l, 32:40, m], scalar=1.0,
                    in1=vecs[:, l * NV_L + 56: l * NV_L + 64], op0=ALU.add, op1=ALU.mult),
                    reads=[T_mod[l], T_const], writes=[T_mod[l]])
            lam_init = 0.8 - 0.6 * math.exp(-0.3 * l)
            P.op("dve", mk("tensor_tensor", out=lamv[:, l, 0:1], in0=vcol(l, 121), in1=vcol(l, 122), op=ALU.mult),
                 reads=[T_const], writes=[T_mod[l]])
            P.op("dve", mk("tensor_tensor", out=lamv[:, l, 1:2], in0=vcol(l, 123), in1=vcol(l, 124), op=ALU.mult),
                 reads=[T_const], writes=[T_mod[l]])
            P.op("pe", mk("matmul", out=ps[1][:, 0:2], lhsT=ones_f, rhs=lamv[:, l, 0:2], start=True, stop=True),
                 reads=[T_mod[l], T_const], writes=[PS[1]])
            P.op("act", mk("activation", out=lamv[:, l, 2:4], in_=ps[1][:, 0:2], func=AF.Exp),
                 reads=[PS[1]], writes=[T_mod[l]])
            P.op("dve", mk("scalar_tensor_tensor",
                out=lamv[:, l, 0:1], in0=lamv[:, l, 3:4], scalar=-lam_init, in1=lamv[:, l, 2:3],
                op0=ALU.add, op1=ALU.subtract), reads=[T_mod[l]], writes=[T_mod[l]])
            P.op("dve", mk("tensor_scalar",
                out=gsub[:, l:l + 1], in0=vcol(l, 120), scalar1=(1.0 - lam_init), scalar2=None, op0=ALU.mult),
                reads=[T_const], writes=[T_mod[l]])

        prologue_n1 = {}

        def prologue():
            T_modA = Tile("modA")
            ph_reset()
            wst0 = [ph_alloc("wada%d" % i, 1024, BF16) for i in range(8)]
            sq, _ = ph_alloc("sq", 8 * 512, BF16)
            nset["sq"] = sq.rearrange("p (k t) -> p k t", k=8)
            nset["t_sq"] = [Tile("sq%d" % k) for k in range(8)]
            ph_tiles(nset["t_sq"])
            nset["rstd"], nset["t_rstd"] = ph_alloc("rstd", 512, F32)
            nset["tmp"] = [ph_alloc("ntmp%d" % i, 512, F32) for i in range(2)]
            ph_commit()
            cnt0 = [0]
            adaln_cols(0, range(16), wst0, cnt0, 0, tile=T_modA)
            adaln_finish_a(0, T_modA)
            rest = list(range(16, 48))

            def between():
                cols = rest[:8]
                del rest[:8]
                adaln_cols(0, cols, wst0, cnt0, 0)
            norm_phase(0, 0, [0, 1, 2, 3, 4], reuse=True, between=between, mod_tile=T_modA)
            adaln_cols(0, list(rest), wst0, cnt0, 0)
            adaln_finish_b(0)
        prologue_n1["fn"] = prologue
        tap("modv", modv.rearrange("p l c m -> p (l c m)"), [T_mod[0]], [128, DEPTH * 96], F32)
        tap("lamv", lamv.rearrange("p l k -> p (l k)"), [T_mod[0]], [128, DEPTH * 4], F32)

        def A_col(l, which, kc, m):
            return derived[:, l, which, kc, m:m + 1]

        def mod_col(l, j, kc, m):
            return modv[:, l, j * 8 + kc, m:m + 1]

        def rms_rstd(src_tiles, src_ap_fn, nk, width, inv_n, sq, t_sq, rstd, t_rstd, bank):
            for kc in range(nk):
                P.op("act", mk("activation", out=sq[:, kc, 0:width], in_=src_ap_fn(kc), func=AF.Square),
                     reads=[src_tiles[kc]], writes=[t_sq[kc]])
            for kc in range(nk):
                P.op("pe", mk("matmul", out=ps[bank][:, 0:width], lhsT=ones_bf, rhs=sq[:, kc, 0:width],
                                                      start=(kc == 0), stop=(kc == nk - 1)),
                     reads=[t_sq[kc], T_const], writes=[PS[bank]])
            P.op("act", mk("activation", out=rstd[:, 0:width], in_=ps[bank][:, 0:width], func=AF.Ln,
                                               scale=inv_n, bias=epsv),
                 reads=[PS[bank], T_const], writes=[t_rstd])
            P.op("act", mk("activation", out=rstd[:, 0:width], in_=rstd[:, 0:width], func=AF.Exp, scale=-0.5),
                 reads=[t_rstd], writes=[t_rstd])

        nset = {}

        def norm_phase(l, which, blocks, reuse=False, commit=True, between=None, mod_tile=None):
            if not (reuse and nset):
                ph_reset()
                sq, _ = ph_alloc("sq", 8 * 512, BF16)
                nset["sq"] = sq.rearrange("p (k t) -> p k t", k=8)
                nset["t_sq"] = [Tile("sq%d" % k) for k in range(8)]
                ph_tiles(nset["t_sq"])
                nset["rstd"], nset["t_rstd"] = ph_alloc("rstd", 512, F32)
                nset["tmp"] = [ph_alloc("ntmp%d" % i, 512, F32) for i in range(2)]
                if commit:
                    ph_commit()
            sq, t_sq, rstd, t_rstd, tmp = nset["sq"], nset["t_sq"], nset["rstd"], nset["t_rstd"], nset["tmp"]
            mt_ = mod_tile or T_mod[l]
            c = 0
            for bi_, b in enumerate(blocks):
                if between is not None and bi_ > 0:
                    between()
                t0, tn = TB[b]
                m = 1 if b == 4 else 0
                rms_rstd([T_x[k][b] for k in range(8)], lambda kc, t0=t0, tn=tn: xT[:, kc, t0:t0 + tn], 8, tn,
                         1.0 / D, sq, t_sq, rstd, t_rstd, 7)
                for kc in range(8):
                    tap_, tt_ = tmp[c % 2]
                    c += 1
                    P.op("dve", mk("scalar_tensor_tensor",
                        out=tap_[:, 0:tn], in0=xT[:, kc, t0:t0 + tn], scalar=A_col(l, which, kc, m),
                        in1=rstd[:, 0:tn], op0=ALU.mult, op1=ALU.mult),
                        reads=[T_x[kc][b], t_rstd, mt_], writes=[tt_])
                    P.op("act", mk("activation",
                        out=hT[:, kc, t0:t0 + tn], in_=tap_[:, 0:tn], func=AF.Identity,
                        bias=mod_col(l, 0 if which == 0 else 3, kc, m), scale=1.0),
                        reads=[tt_, mt_], writes=[T_h[kc][b]])

        for l in range(n_layers):
            last = (l == DEPTH - 1)
            qblocks = [0, 1, 2, 3] if last else [0, 1, 2, 3, 4]
            vb = l * NV_L
            if l == 0:
                prologue_n1["fn"]()
            else:
                norm_phase(l, 0, [0, 1, 2, 3, 4], reuse=True)
            if l == 0:
                tap("h0", hT.rearrange("p k t -> p (k t)"), [T_h[k][b] for k in range(8) for b in range(5)], [128, 8 * NTOK], BF16)
            if stop_after == "n1":
                break
            ph_reset()
            qh, _ = ph_alloc("qh", NTOK, BF16)
            kh, _ = ph_alloc("kh", NTOK, BF16)
            vh, _ = ph_alloc("vh", 18 * 128, BF16)
            vh = vh.rearrange("p (t n) -> p t n", t=18)
            T_q = [Tile("q%d" % b) for b in range(5)]
            T_k = [Tile("k%d" % b) for b in range(5)]
            T_v = [Tile("v%d" % t) for t in range(18)]
            ph_tiles(T_q + T_k + T_v)
            pbuf = [ph_alloc("p%d" % i, 1024, BF16) for i in range(3)]
            r1, t_r1 = ph_alloc("r1", 512, F32)
            r2, t_r2 = ph_alloc("r2", 512, F32)
            ob, t_ob = ph_alloc("ob", 512, F32)
            osq, t_osq = ph_alloc("osq", 512, BF16)
            bvt, t_bvt = ph_alloc("bvt", 128, F32)
            accs = [ph_alloc("acc%d" % i, 512, F32) for i in range(2)]
            ph_commit()
            P.alias([T_rope], [T_yf[k][b] for k in range(2) for b in range(5)] + [T_yc[k][b] for k in range(2) for b in range(5)])
            t_p3 = Tile("p3")
            P.alias([t_p3], [T_yf[k][b] for k in range(2) for b in range(5)] + [T_yc[k][b] for k in range(2) for b in range(5)])
            pbuf = pbuf + [(view(O_BR + 4 * NTOK * 2 + 2 * L * 4, 1024, BF16), t_p3)]
            P.dma("sp", mk("dma_start", out=ropeT.rearrange("p k t -> p (k t)"), in_=rope_d), writes=[T_rope])
            pcount = 0
            for h in range(4):
                wqk, t_wqk = w_next("attn_qk", h)
                wqk3 = wqk[:, 0:2048].rearrange("p (k n) -> p k n", k=8)
                bq = [vcol(l, 64 + h * 4 + i) for i in range(4)]
                ktmp = accs[1][0].bitcast(BF16)
                t_ktmp = accs[1][1]
                for b in range(5):
                    t0, tn = TB[b]
                    need_q = b in qblocks
                    base = 0 if b % 2 == 0 else 4
                    todo = [(dst, T_d, base + c0, c0, wc, tmp_, t_tmp_) for (dst, T_d, c0, wc, tmp_, t_tmp_) in
                            ((qh, T_q, 0, 0, osq, t_osq), (kh, T_k, 2, 1, ktmp, t_ktmp)) if (c0 != 0 or need_q)]
                    for (dst, T_d, bk, c0, wc, tmp_, t_tmp_) in todo:
                        for kc in range(8):
                            P.op("pe", mk("matmul",
                                out=ps[bk][:, 0:tn], lhsT=wqk3[:, kc, wc * 128:(wc + 1) * 128],
                                rhs=hT[:, kc, t0:t0 + tn], start=(kc == 0), stop=(kc == 7)),
                                reads=[t_wqk, T_h[kc][b]], writes=[PS[bk]])
                        if b < 4:
                            P.op("act", mk("activation",
                                out=tmp_[:, 0:tn], in_=ps[bk][:, 0:tn], func=AF.Identity, bias=bq[c0], scale=1.0),
                                reads=[PS[bk], T_const], writes=[t_tmp_])
                        else:
                            P.op("act", mk("activation",
                                out=dst[:, t0:t0 + tn], in_=ps[bk][:, 0:tn], func=AF.Identity, bias=bq[c0], scale=1.0),
                                reads=[PS[bk], T_const], writes=[T_d[b]])
                    if b < 4:
                        for (dst, T_d, bk, c0, wc, tmp_, t_tmp_) in todo:
                            P.op("pe", mk("matmul", out=ps[bk + 1][:, 0:tn], lhsT=permM, rhs=tmp_[:, 0:tn], start=True, stop=True),
                                 reads=[t_tmp_, T_const], writes=[PS[bk + 1]])
                        for (dst, T_d, bk, c0, wc, tmp_, t_tmp_) in todo:
                            P.op("dve", mk("scalar_tensor_tensor",
                                out=r1[:, 0:tn], in0=ps[bk][:, 0:tn], scalar=bq[c0], in1=ropeT[:, 0, t0:t0 + tn],
                                op0=ALU.add, op1=ALU.mult), reads=[PS[bk], T_rope, T_const, t_tmp_], writes=[t_r1])
                            P.op("dve", mk("tensor_tensor",
                                out=r2[:, 0:tn], in0=ps[bk + 1][:, 0:tn], in1=ropeT[:, 1, t0:t0 + tn], op=ALU.mult),
                                reads=[PS[bk + 1], T_rope], writes=[t_r2])
                            P.op("dve", mk("tensor_tensor",
                                out=dst[:, t0:t0 + tn], in0=r1[:, 0:tn], in1=r2[:, 0:tn], op=ALU.add),
                                reads=[t_r1, t_r2], writes=[T_d[b]])
                P.dma("sp", mk("dma_start", out=bvt, in_=bvb_d[l][:, h * 128:(h + 1) * 128]), writes=[t_bvt])
                wv, t_wv = w_next("attn_v", h)
                wv3 = wv[:, 0:1024].rearrange("p (k n) -> p k n", k=8)
                for tt in range(18):
                    b = min(tt // 4, 4)
                    bank = 4 + (tt % 2)
                    for kc in range(8):
                        P.op("pe", mk("matmul",
                            out=ps[bank][:, 0:128], lhsT=hT[:, kc, tt * 128:(tt + 1) * 128], rhs=wv3[:, kc, :],
                            start=(kc == 0), stop=(kc == 7)), reads=[t_wv, T_h[kc][b]], writes=[PS[bank]])
                    P.op("dve", mk("tensor_tensor",
                        out=vh[:, tt, :], in0=ps[bank][:, 0:128], in1=bvt[:, 0:128], op=ALU.add),
                        reads=[PS[bank], t_bvt], writes=[T_v[tt]])
                if l == 0:
                    tap("q%d" % h, qh, T_q, [128, NTOK], BF16)
                    tap("k%d" % h, kh, T_k, [128, NTOK], BF16)
                    tap("v%d" % h, vh.rearrange("p t n -> p (t n)"), T_v, [128, 18 * 128], BF16)
                steps = []
                for qb in qblocks:
                    kts = list(range(18)) if qb < 4 else [16, 17]
                    for i, kt in enumerate(kts):
                        steps.append((qb, i, kt, len(kts)))

                def emit_qk(si):
                    qb, i, kt, n = steps[si]
                    q0, qn = TB[qb]
                    kb = min(kt // 4, 4)
                    sb = (si % 2) * 2
                    for c in range(2):
                        P.op("pe", mk("matmul",
                            out=ps[sb + c][:, 0:qn], lhsT=kh[c * 64:(c + 1) * 64, kt * 128:(kt + 1) * 128],
                            rhs=qh[c * 64:(c + 1) * 64, q0:q0 + qn], start=True, stop=True),
                            reads=[T_k[kb], T_q[qb]], writes=[PS[sb + c]])

                pending = []
                pending1 = []
                acc0, t_acc0 = accs[0]
                csum3 = accs[1][0].bitcast(BF16).rearrange("p (c n) -> p c n", c=2)
                t_csum = accs[1][1]
                prev_p = None
                emit_qk(0)
                if len(steps) > 1:
                    emit_qk(1)
                for si, (qb, i, kt, nkt) in enumerate(steps):
                    q0, qn = TB[qb]
                    sb = (si % 2) * 2
                    pa2, pt = pbuf[pcount % 4]
                    pcount += 1
                    pa3 = pa2.rearrange("p (c n) -> p c n", c=2)
                    P.op("act", mk("activation",
                        out=pa3[:, :, 0:qn],
                        in_=psum_all[:, sb * 512:(sb + 2) * 512].rearrange("p (c n) -> p c n", c=2)[:, :, 0:qn],
                        func=AF.Exp, scale=0.125), reads=[PS[sb], PS[sb + 1]], writes=[pt])
                    if si + 2 < len(steps):
                        emit_qk(si + 2)
                    for c in range(2):
                        P.op("pe", mk("matmul",
                            out=ps[4 + c][:, 0:qn], lhsT=vh[:, kt, :], rhs=pa3[:, c, 0:qn],
                            start=(i == 0), stop=(i == nkt - 1)), reads=[T_v[kt], pt], writes=[PS[4 + c]])
                    if i == 1:
                        P.op("dve", mk("tensor_tensor", out=csum3[:, :, 0:qn], in0=prev_p[0][:, :, 0:qn], in1=pa3[:, :, 0:qn], op=ALU.add),
                             reads=[prev_p[1], pt], writes=[t_csum])
                    elif i > 1:
                        P.op("dve", mk("tensor_tensor", out=csum3[:, :, 0:qn], in0=csum3[:, :, 0:qn], in1=pa3[:, :, 0:qn], op=ALU.add),
                             reads=[t_csum, pt], writes=[t_csum])
                    prev_p = (pa3, pt)
                    if pending1 and (i == min(1, nkt - 1)):
                        pending1.pop()()
                    if pending and (i == min(3, nkt - 1)):
                        pending.pop()(6)
                    if i == nkt - 1:
                        for c in range(2):
                            P.op("pe", mk("matmul", out=ps[6 + c][:, 0:qn], lhsT=ones_bf, rhs=csum3[:, c, 0:qn], start=True, stop=True),
                                 reads=[T_const, t_csum], writes=[PS[6 + c]])
                    if i != nkt - 1:
                        continue
                    P.op("dve", mk("tensor_copy", out=r1[:, 0:qn], in_=ps[4][:, 0:qn]), reads=[PS[4]], writes=[t_r1])
                    P.op("dve", mk("tensor_copy", out=r2[:, 0:qn], in_=ps[5][:, 0:qn]), reads=[PS[5]], writes=[t_r2])

                    def part1b(qn=qn, q0=q0, h=h, qb=qb):
                        P.op("act", mk("activation", out=acc0[:, 0:qn], in_=ps[6][:, 0:qn], func=AF.Ln), reads=[PS[6]], writes=[t_acc0])
                        P.op("act", mk("activation", out=ob[:, 0:qn], in_=ps[7][:, 0:qn], func=AF.Ln), reads=[PS[7]], writes=[t_ob])
                        P.op("act", mk("activation", out=acc0[:, 0:qn], in_=acc0[:, 0:qn], func=AF.Exp, scale=-1.0), reads=[t_acc0], writes=[t_acc0])
                        P.op("act", mk("activation", out=ob[:, 0:qn], in_=ob[:, 0:qn], func=AF.Exp, scale=-1.0), reads=[t_ob], writes=[t_ob])
                        P.op("dve", mk("tensor_tensor", out=r1[:, 0:qn], in0=r1[:, 0:qn], in1=acc0[:, 0:qn], op=ALU.mult),
                             reads=[t_r1, t_acc0], writes=[t_r1])
                        P.op("dve", mk("tensor_tensor", out=r2[:, 0:qn], in0=r2[:, 0:qn], in1=ob[:, 0:qn], op=ALU.mult),
                             reads=[t_r2, t_ob], writes=[t_r2])
                        P.op("dve", mk("scalar_tensor_tensor",
                            out=ob[:, 0:qn], in0=r2[:, 0:qn], scalar=lamv[:, l, 0:1], in1=r1[:, 0:qn],
                            op0=ALU.mult, op1=ALU.add), reads=[t_r1, t_r2, T_mod[l]], writes=[t_ob])
                        P.op("dve", mk("tensor_tensor", out=osq[:, 0:qn], in0=ob[:, 0:qn], in1=ob[:, 0:qn], op=ALU.mult),
                             reads=[t_ob], writes=[t_osq])

                        def part2(bank):
                            P.op("pe", mk("matmul", out=ps[bank][:, 0:qn], lhsT=ones_bf, rhs=osq[:, 0:qn], start=True, stop=True),
                                 reads=[t_osq, T_const], writes=[PS[bank]])
                            P.op("act", mk("activation", out=acc0[:, 0:qn], in_=ps[bank][:, 0:qn], func=AF.Ln,
                                           scale=1.0 / 128.0, bias=epsv), reads=[PS[bank], T_const], writes=[t_acc0])
                            P.op("act", mk("activation", out=acc0[:, 0:qn], in_=acc0[:, 0:qn], func=AF.Exp, scale=-0.5),
                                 reads=[t_acc0], writes=[t_acc0])
                            P.op("dve", mk("scalar_tensor_tensor",
                                out=oT[:, h, q0:q0 + qn], in0=ob[:, 0:qn], scalar=gsub[:, l:l + 1], in1=acc0[:, 0:qn],
                                op0=ALU.mult, op1=ALU.mult), reads=[t_ob, t_acc0, T_mod[l]], writes=[T_o[h][qb]])
                        pending.append(part2)
                    pending1.append(part1b)
                if pending1:
                    pending1.pop()()
                if pending:
                    pending.pop()(6)
            if l == 0:
                tap("o0", oT.rearrange("p k t -> p (k t)"), [T_o[k][b] for k in range(4) for b in range(5)], [128, 4 * NTOK], BF16)
            if stop_after == "attn":
                break
            ph_reset()
            wf, t_wf = w_next("four", 0)
            wf3 = wf[:, 0:2048].rearrange("p (k n) -> p k n", k=8)
            uT, _ = ph_alloc("uT", NTOK, BF16)
            T_u = [Tile("u%d" % b) for b in range(5)]
            ucs, _ = ph_alloc("ucs", 18 * 512, BF16)
            ucs = ucs.rearrange("p (t c s n) -> p t c s n", t=18, c=2, s=2)
            T_ucs = [[Tile("ucs%d_%d" % (t, c)) for c in range(2)] for t in range(18)]
            ph_tiles(T_u + [x for y in T_ucs for x in y])
            dring = [ph_alloc("dring%d" % i, 2048, BF16) for i in range(2)]
            ph_commit()
            P.alias([T_yf[k][b] for k in range(2) for b in range(5)] + [T_yc[k][b] for k in range(2) for b in range(5)], [T_rope, t_p3])
            ntt = 18 if not last else 16
            dcount = 0
            for cc in range(2):
                for b in qblocks:
                    t0, tn = TB[b]
                    for kc in range(8):
                        P.op("pe", mk("matmul",
                            out=ps[0][:, 0:tn], lhsT=wf3[:, kc, cc * 128:(cc + 1) * 128], rhs=hT[:, kc, t0:t0 + tn],
                            start=(kc == 0), stop=(kc == 7)), reads=[t_wf, T_h[kc][b]], writes=[PS[0]])
                    P.op("act", mk("activation",
                        out=uT[:, t0:t0 + tn], in_=ps[0][:, 0:tn], func=AF.Identity, bias=vcol(l, 80 + cc), scale=1.0),
                        reads=[PS[0], T_const], writes=[T_u[b]])
                for tt in range(ntt):
                    b = min(tt // 4, 4)
                    bank = 1 + (tt % 2)
                    P.op("pe", mk("matmul",
                        out=ps[bank][:, 0:256], lhsT=uT[:, tt * 128:(tt + 1) * 128], rhs=dftch, start=True, stop=True),
                        reads=[T_u[b], T_const], writes=[PS[bank]])
                    P.op("dve", mk("tensor_copy",
                        out=ucs[:, tt, cc, :, :].rearrange("p s n -> p (s n)"), in_=ps[bank][:, 0:256]),
                        reads=[PS[bank]], writes=[T_ucs[tt][cc]])
            for j in range(4):
                for piece in range(8):
                    da, dt_ = dring[dcount % 2]
                    dcount += 1
                    P.dma("sp", mk("dma_start", out=da, in_=dftL_d[j, piece]), writes=[dt_])
                    d4 = da.rearrange("p (c t n) -> p c t n", c=2, t=2)
                    for tl in range(2):
                        tt = piece * 2 + tl
                        for cc in range(2):
                            for cs in range(2):
                                P.op("pe", mk("matmul",
                                    out=ps[3 + cc][:, :], lhsT=ucs[:, tt, cc, cs, :], rhs=d4[:, cs, tl, :],
                                    start=(tt == 0 and cs == 0), stop=(tt == 15 and cs == 1)),
                                    reads=[T_ucs[tt][cc], dt_], writes=[PS[3 + cc]])
                for cc in range(2):
                    P.op("act", mk("copy", out=yfT[:, cc, j * 512:(j + 1) * 512], in_=ps[3 + cc][:, :]),
                         reads=[PS[3 + cc]], writes=[T_yf[cc][j]])
            if not last:
                for cc in range(2):
                    for tl in range(2):
                        for cs in range(2):
                            P.op("pe", mk("matmul",
                                out=ps[5][:, 0:256], lhsT=ucs[:, 16 + tl, cc, cs, :], rhs=dftC[:, cs, tl, :],
                                start=(tl == 0 and cs == 0), stop=(tl == 1 and cs == 1)),
                                reads=[T_ucs[16 + tl][cc], T_const], writes=[PS[5]])
                    P.op("act", mk("copy", out=yfT[:, cc, 2048:2304], in_=ps[5][:, 0:256]),
                         reads=[PS[5]], writes=[T_yf[cc][4]])
            if l == 0:
                tap("yf0", yfT.rearrange("p k t -> p (k t)"), [T_yf[k][b] for k in range(2) for b in range(5)], [128, 2 * NTOK], BF16)
            if stop_after == "four":
                break
            ph_reset()
            ZW = 2312
            zb, _ = ph_alloc("zb", ZW, F32)
            T_z = [Tile("z%d" % b) for b in range(5)]
            t_zpad = Tile("zpad")
            ph_tiles(T_z + [t_zpad])
            cgs, t_cgs = ph_alloc("cgs", 512, F32)
            c1, t_c1 = ph_alloc("c1", 512, F32)
            c2, t_c2 = ph_alloc("c2", 512, F32)
            defer_ada = (l + 1 < n_layers)
            if defer_ada:
                ada_slots = [ph_alloc("adc%d" % i, 1024, BF16) for i in range(8)]
            ph_commit()
            ada_pairs = ada_begin(l + 1, ada_slots) if defer_ada else []
            wcx, t_wcx = w_next("conv_cx", 0)
            wcx3 = wcx.rearrange("p (k n) -> p k n", k=8)
            wcb, t_wcb = w_next("conv_b", 0)
            wcb3 = wcb[:, 0:2048].rearrange("p (k n) -> p k n", k=8)
            zoff = [1, 513, 1025, 1537, 2051]
            for padc in (0, 2049, 2050, 2307):
                P.op("dve", mk("memset", zb[:, padc:padc + 1], 0.0), writes=[t_zpad])
            for cc in range(2):
                for b in qblocks:
                    t0, tn = TB[b]
                    for ci in range(2):
                        for kc in range(8):
                            P.op("pe", mk("matmul",
                                out=ps[ci][:, 0:tn], lhsT=wcx3[:, kc, (cc * 2 + ci) * 128:(cc * 2 + ci + 1) * 128],
                                rhs=hT[:, kc, t0:t0 + tn], start=(kc == 0), stop=(kc == 7)),
                                reads=[t_wcx, T_h[kc][b]], writes=[PS[ci]])
                    P.op("act", mk("activation",
                        out=cgs[:, 0:tn], in_=ps[0][:, 0:tn], func=AF.Identity, bias=vcol(l, 82 + cc * 2), scale=1.0),
                        reads=[PS[0], T_const], writes=[t_cgs])
                    P.op("dve", mk("scalar_tensor_tensor",
                        out=zb[:, zoff[b]:zoff[b] + tn], in0=ps[1][:, 0:tn], scalar=vcol(l, 83 + cc * 2), in1=cgs[:, 0:tn],
                        op0=ALU.add, op1=ALU.mult), reads=[PS[1], t_cgs, T_const], writes=[T_z[b]])
                for b in qblocks:
                    t0, tn = TB[b]
                    zo = zoff[b]
                    nb = [T_z[x] for x in (b - 1, b, b + 1) if 0 <= x < 4 and b < 4] if b < 4 else [T_z[4]]
                    for kc in range(8):
                        P.op("pe", mk("matmul",
                            out=ps[2][:, 0:tn], lhsT=wcb3[:, kc, cc * 128:(cc + 1) * 128], rhs=hT[:, kc, t0:t0 + tn],
                            start=(kc == 0), stop=(kc == 7)), reads=[t_wcb, T_h[kc][b]], writes=[PS[2]])
                    P.op("dve", mk("tensor_scalar",
                        out=c1[:, 0:tn], in0=zb[:, zo - 1:zo - 1 + tn], scalar1=vcol(l, 112 + cc), scalar2=vcol(l, 118 + cc),
                        op0=ALU.mult, op1=ALU.add), reads=nb + [t_zpad, T_const], writes=[t_c1])
                    P.op("dve", mk("scalar_tensor_tensor",
                        out=c2[:, 0:tn], in0=zb[:, zo:zo + tn], scalar=vcol(l, 114 + cc), in1=c1[:, 0:tn],
                        op0=ALU.mult, op1=ALU.add), reads=nb + [t_c1, T_const], writes=[t_c2])
                    P.op("dve", mk("scalar_tensor_tensor",
                        out=c1[:, 0:tn], in0=zb[:, zo + 1:zo + 1 + tn], scalar=vcol(l, 116 + cc), in1=c2[:, 0:tn],
                        op0=ALU.mult, op1=ALU.add), reads=nb + [t_zpad, t_c2, T_const], writes=[t_c1])
                    P.op("dve", mk("scalar_tensor_tensor",
                        out=ycT[:, cc, t0:t0 + tn], in0=ps[2][:, 0:tn], scalar=vcol(l, 86 + cc), in1=c1[:, 0:tn],
                        op0=ALU.add, op1=ALU.mult), reads=[PS[2], t_c1, T_const], writes=[T_yc[cc][b]])
            ada_end(l + 1, ada_pairs, 0)
            if l == 0:
                tap("yc0", ycT.rearrange("p k t -> p (k t)"), [T_yc[k][b] for k in range(2) for b in range(5)], [128, 2 * NTOK], BF16)
            if stop_after == "conv":
                break
            ph_reset()
            sig = [ph_alloc("sig%d" % i, 512, F32) for i in range(3)]
            mm = [ph_alloc("mm%d" % i, 512, F32) for i in range(3)]
            ybs = [[ph_alloc("yb%d_%d" % (p_, i), 512, BF16) for i in range(2)] for p_ in range(2)]
            if defer_ada:
                ada_slots = [ph_alloc("adm%d" % i, 1024, BF16) for i in range(7)]
            ph_commit()
            mpend = []
            gb_cnt = 0
            for g in range(4):
                ada_pairs = ada_begin(l + 1, ada_slots) if defer_ada else []
                wgfc, t_wgfc = w_next("gate_fc", g)
                wgfc3 = wgfc.rearrange("p (k n) -> p k n", k=8)
                wga, t_wga = w_next("gate_a_wo", g)
                wga3 = wga[:, 0:2048].rearrange("p (k n) -> p k n", k=8)
                wo3 = wga[:, 2048:4096].rearrange("p (k n) -> p k n", k=8)
                ww, t_ww = w_next("wout", g)
                ww3 = ww[:, 0:2048].rearrange("p (k n) -> p k n", k=2)
                for b in qblocks:
                    t0, tn = TB[b]
                    m = 1 if b == 4 else 0
                    yb = ybs[gb_cnt % 2]
                    gb_cnt += 1
                    for dd in range(2):
                        db = g * 2 + dd
                        for br in range(3):
                            for kc in range(8):
                                if br < 2:
                                    lh = wgfc3[:, kc, br * 256 + dd * 128: br * 256 + (dd + 1) * 128]
                                    tw = t_wgfc
                                else:
                                    lh = wga3[:, kc, dd * 128:(dd + 1) * 128]
                                    tw = t_wga
                                P.op("pe", mk("matmul",
                                    out=ps[br][:, 0:tn], lhsT=lh, rhs=hT[:, kc, t0:t0 + tn], start=(kc == 0), stop=(kc == 7)),
                                    reads=[tw, T_h[kc][b]], writes=[PS[br]])
                            sa, st_ = sig[br]
                            P.op("act", mk("activation",
                                out=sa[:, 0:tn], in_=ps[br][:, 0:tn], func=AF.Sigmoid, bias=vcol(l, 88 + br * 8 + db), scale=1.0),
                                reads=[PS[br], T_const], writes=[st_])
                        if dd == 1 and mpend:
                            mpend.pop()()
                        srcs = [(0, [yfT[:, 0, t0:t0 + tn], yfT[:, 1, t0:t0 + tn]], [T_yf[0][b], T_yf[1][b]]),
                                (2, [ycT[:, 0, t0:t0 + tn], ycT[:, 1, t0:t0 + tn]], [T_yc[0][b], T_yc[1][b]]),
                                (4, [oT[:, hh, t0:t0 + tn] for hh in range(4)], [T_o[hh][b] for hh in range(4)])]
                        for br, (k0, rl, tl_) in enumerate(srcs):
                            for i, (ra, rt) in enumerate(zip(rl, tl_)):
                                P.op("pe", mk("matmul",
                                    out=ps[3 + br][:, 0:tn], lhsT=wo3[:, k0 + i, dd * 128:(dd + 1) * 128], rhs=ra,
                                    start=(i == 0), stop=(i == len(rl) - 1)), reads=[t_wga, rt], writes=[PS[3 + br]])
                            sa, st_ = sig[br]
                            ma, mt = mm[br]
                            P.op("dve", mk("tensor_tensor",
                                out=ma[:, 0:tn], in0=ps[3 + br][:, 0:tn], in1=sa[:, 0:tn], op=ALU.mult),
                                reads=[PS[3 + br], st_], writes=[mt])
                        P.op("dve", mk("tensor_tensor", out=mm[0][0][:, 0:tn], in0=mm[0][0][:, 0:tn], in1=mm[1][0][:, 0:tn], op=ALU.add),
                             reads=[mm[0][1], mm[1][1]], writes=[mm[0][1]])
                        P.op("dve", mk("tensor_tensor", out=yb[dd][0][:, 0:tn], in0=mm[0][0][:, 0:tn], in1=mm[2][0][:, 0:tn], op=ALU.add),
                             reads=[mm[0][1], mm[2][1]], writes=[yb[dd][1]])
                    def wout_fn(b=b, t0=t0, tn=tn, m=m, yb=yb, ww3=ww3, t_ww=t_ww):
                        for d2 in range(8):
                            bank = 6 + (d2 % 2)
                            for dd in range(2):
                                P.op("pe", mk("matmul",
                                    out=ps[bank][:, 0:tn], lhsT=ww3[:, dd, d2 * 128:(d2 + 1) * 128], rhs=yb[dd][0][:, 0:tn],
                                    start=(dd == 0), stop=(dd == 1)), reads=[t_ww, yb[dd][1]], writes=[PS[bank]])
                            P.op("dve", mk("scalar_tensor_tensor",
                                out=xT[:, d2, t0:t0 + tn], in0=ps[bank][:, 0:tn], scalar=mod_col(l, 2, d2, m), in1=xT[:, d2, t0:t0 + tn],
                                op0=ALU.mult, op1=ALU.add), reads=[PS[bank], T_x[d2][b], T_mod[l]], writes=[T_x[d2][b]])
                    assert not mpend
                    mpend.append(wout_fn)
                ada_end(l + 1, ada_pairs, 0)
            if mpend:
                mpend.pop()()
            if l == 0:
                tap("x1", xT.rearrange("p k t -> p (k t)"), [T_x[k][b] for k in range(8) for b in range(5)], [128, 8 * NTOK], F32)
            if stop_after == "merge":
                break
            norm_phase(l, 1, qblocks, commit=False)
            sg = [ph_alloc("sg%d" % i, 512, F32) for i in range(2)]
            act = [ph_alloc("act%d" % i, 512, BF16) for i in range(8)]
            if last:
                ost = [ph_alloc("ost%d" % i, 512, F32) for i in range(2)]
            if defer_ada:
                ada_slots = [ph_alloc("adf%d" % i, 1024, BF16) for i in range(2)]
            ph_commit()
            acount = 0
            scount = 0
            for fg in range(6):
                nf = 4 if fg < 5 else 2
                ada_pairs = ada_begin(l + 1, ada_slots) if defer_ada else []
                wg_, t_wg = w_next("ffn_g", fg)
                wg3 = wg_[:, 0:8 * nf * 128].rearrange("p (k n) -> p k n", k=8)
                wu_, t_wu = w_next("ffn_u", fg)
                wu3 = wu_[:, 0:8 * nf * 128].rearrange("p (k n) -> p k n", k=8)
                wd_, t_wd = w_next("ffn_d", fg)
                wd3 = wd_[:, 0:nf * 1024].rearrange("p (k n) -> p k n", k=nf)
                for b in qblocks:
                    t0, tn = TB[b]
                    m = 1 if b == 4 else 0
                    acts = []
                    for fb in range(nf):
                        bg_, bu_ = (fb % 2) * 2, (fb % 2) * 2 + 1
                        for (w3, tw, bank) in ((wg3, t_wg, bg_), (wu3, t_wu, bu_)):
                            for kc in range(8):
                                P.op("pe", mk("matmul",
                                    out=ps[bank][:, 0:tn], lhsT=w3[:, kc, fb * 128:(fb + 1) * 128], rhs=hT[:, kc, t0:t0 + tn],
                                    start=(kc == 0), stop=(kc == 7)), reads=[tw, T_h[kc][b]], writes=[PS[bank]])
                        sa, st_ = sg[scount % 2]
                        scount += 1
                        aa, at = act[acount % 8]
                        acount += 1
                        acts.append((aa, at))
                        P.op("act", mk("activation", out=sa[:, 0:tn], in_=ps[bg_][:, 0:tn], func=AF.Silu),
                             reads=[PS[bg_]], writes=[st_])
                        P.op("dve", mk("tensor_tensor",
                            out=aa[:, 0:tn], in0=ps[bu_][:, 0:tn], in1=sa[:, 0:tn], op=ALU.mult),
                            reads=[PS[bu_], st_], writes=[at])
                    for d2 in range(8):
                        bank = 4 + (d2 % 4)
                        for fb in range(nf):
                            aa, at = acts[fb]
                            P.op("pe", mk("matmul",
                                out=ps[bank][:, 0:tn], lhsT=wd3[:, fb, d2 * 128:(d2 + 1) * 128], rhs=aa[:, 0:tn],
                                start=(fb == 0), stop=(fb == nf - 1)), reads=[t_wd, at], writes=[PS[bank]])
                        P.op("dve", mk("scalar_tensor_tensor",
                            out=xT[:, d2, t0:t0 + tn], in0=ps[bank][:, 0:tn], scalar=mod_col(l, 5, d2, m), in1=xT[:, d2, t0:t0 + tn],
                            op0=ALU.mult, op1=ALU.add), reads=[PS[bank], T_x[d2][b], T_mod[l]], writes=[T_x[d2][b]])
                ada_end(l + 1, ada_pairs, 0)
            if defer_ada:
                assert ada_state["next"] == 48, ada_state
                ada_state["next"] = 0
                adaln_finish(l + 1)
            if l == 0:
                tap("x2", xT.rearrange("p k t -> p (k t)"), [T_x[k][b] for k in range(8) for b in range(5)], [128, 8 * NTOK], F32)

        T_out = Tile("out")
        if stop_after is None:
            sq, t_sq, rstd, t_rstd = nset["sq"], nset["t_sq"], nset["rstd"], nset["t_rstd"]
            c = 0
            fgo = DEPTH * NV_L
            for b in range(4):
                t0, tn = TB[b]
                rms_rstd([T_x[k][b] for k in range(8)], lambda kc, t0=t0, tn=tn: xT[:, kc, t0:t0 + tn], 8, tn,
                         1.0 / D, sq, t_sq, rstd, t_rstd, 7)
                for kc in range(8):
                    oa, ot = ost[c % 2]
                    c += 1
                    P.op("dve", mk("scalar_tensor_tensor",
                        out=oa[:, 0:tn], in0=xT[:, kc, t0:t0 + tn], scalar=vecs[:, fgo + kc: fgo + kc + 1],
                        in1=rstd[:, 0:tn], op0=ALU.mult, op1=ALU.mult),
                        reads=[T_x[kc][b], t_rstd, T_const], writes=[ot])
                    P.dma("sp", mk("dma_start",
                        out=out_d[kc * 128:(kc + 1) * 128, t0:t0 + tn], in_=oa[:, 0:tn]),
                        reads=[ot], writes=[T_out], sem_tile=ot)
            P.final_wait("sp", [T_out] + [t for _, t in ost])
        else:
            P.op("dve", mk("memset", xT[:, 0, 0:512], 0.0), writes=[T_out])
            P.dma("sp", mk("dma_start", out=out_d[0:128, 0:512], in_=xT[:, 0, 0:512]), reads=[T_out], writes=[T_out])
            P.final_wait("sp", [T_out])
        P.final_wait("sp", list(dbg_out.values()))
        P.emit()
    return nc


def make_in_maps(inputs):
    inp = {k: np.asarray(v) for k, v in inputs.items()}
    ct = _const_tables()
    wb = np.stack([_pack_weights(inp, l) for l in range(DEPTH)], axis=0)
    wada = np.stack([
        np.stack([_tile_kxn(inp["w_ada"][l][:, g * 128:(g + 1) * 128]) for g in range(48)], axis=0)
        for l in range(DEPTH)], axis=0).astype(np.float32)
    vecs = _pack_vecs(inp)
    bvb = np.stack([np.broadcast_to(inp["b_in"][l][V_OFF:V_OFF + 512][None, :], (128, 512)) for l in range(DEPTH)],
                   axis=0).astype(np.float32)
    bvb = np.ascontiguousarray(bvb)
    rope = np.ascontiguousarray(ct["rope"].reshape(128, 2 * L))
    cctx = _col(inp["c_ctx"])
    maps = []
    for b in range(8):
        xt = np.ascontiguousarray(np.concatenate([inp["x"][b].T, inp["ctx"][b].T], axis=1), dtype=np.float32)
        cv = np.ascontiguousarray(np.stack([_col(inp["c"][b]), cctx], axis=2).reshape(128, 16), dtype=np.float32)
        maps.append({"xt": xt, "cv": cv, "wada": wada, "wb": wb, "vecs": vecs, "bvb": bvb, "rope": rope,
                     "dftL": ct["dftL"], "dftC": ct["dftC"], "dftch": ct["dftch"], "permm": ct["permm"]})
    return maps


_NC_CACHE = {}


def kernel(**inputs):
    maps = make_in_maps(inputs)
    if "nc" not in _NC_CACHE:
        _NC_CACHE["nc"] = build_program()
    res = run_bass_kernel_spmd(_NC_CACHE["nc"], maps, core_ids=list(range(8)))
    out = np.stack([np.asarray(r["outT"]).T for r in res.results], axis=0)
    return np.ascontiguousarray(out, dtype=np.float32)
```

```python
import math
import contextlib
import numpy as np
import ml_dtypes
import concourse.bass as bass
import concourse.mybir as mybir
from concourse.bass_utils import run_bass_kernel_spmd

F32 = mybir.dt.float32
BF16 = mybir.dt.bfloat16
AF = mybir.ActivationFunctionType
ALU = mybir.AluOpType

D = 1024
L = 2048
CTX = 256
NTOK = L + CTX
DEPTH = 2
DFF = 2816
F_OFF, CB_OFF, CC_OFF, CX_OFF, Q_OFF, K_OFF, V_OFF, G_OFF = 0, 256, 512, 768, 1024, 1536, 2048, 2560
EPS = 1e-6
TB = [(0, 512), (512, 512), (1024, 512), (1536, 512), (2048, 256)]
NV_L = 125
ENGS = ("pe", "act", "dve", "pool", "sp")


class Tile:
    __slots__ = ("name", "lw", "rd", "dma_sem", "dma_cnt", "hist")

    def __init__(self, name):
        self.name = name
        self.lw = None
        self.rd = []
        self.dma_sem = None
        self.dma_cnt = 0
        self.hist = []


class Prog:
    def __init__(self, nc):
        self.nc = nc
        self.ops = {e: [] for e in ENGS}
        self.n_dma_sem = 0

    def _deps(self, eng, reads, writes):
        deps = []
        for t in reads:
            if t.lw is not None:
                deps.append(t.lw)
        for t in writes:
            if t.lw is not None:
                deps.append(t.lw)
            deps.extend(t.rd)
        out = []
        seen = set()
        for d in deps:
            if d in seen:
                continue
            seen.add(d)
            if d[0] == "eng" and d[1] == eng and eng == "pe":
                continue
            out.append(d)
        return out

    def op(self, eng, fn, reads=(), writes=()):
        deps = self._deps(eng, reads, writes)
        ev = ("eng", eng, len(self.ops[eng]))
        self.ops[eng].append({"fn": fn, "deps": deps, "ev": ev, "kind": "c", "lazy": ()})
        for t in reads:
            t.rd.append(ev)
            t.hist.append(ev)
        for t in writes:
            t.lw = ev
            t.rd = []
            t.hist.append(ev)
        return ev

    def dma(self, queue, fn, reads=(), writes=(), sem_tile=None, lazy=()):
        deps = self._deps(queue, reads, writes)
        st = sem_tile or (writes[0] if writes else reads[0])
        if st.dma_sem is None:
            st.dma_sem = self.n_dma_sem
            self.n_dma_sem += 1
        st.dma_cnt += 16
        ev = ("dma", st.dma_sem, st.dma_cnt)
        self.ops[queue].append({"fn": fn, "deps": deps, "ev": ev, "kind": "d", "lazy": tuple(lazy)})
        for t in reads:
            t.rd.append(ev)
            t.hist.append(ev)
        for t in writes:
            t.lw = ev
            t.rd = []
            t.hist.append(ev)
        return ev

    def alias(self, new_tiles, old_tiles):
        ev = []
        for t in old_tiles:
            if t.lw is not None:
                ev.append(t.lw)
            ev.extend(t.rd)
        ev = list(dict.fromkeys(ev))
        for t in new_tiles:
            t.rd = list(dict.fromkeys(list(t.rd) + ev))

    def final_wait(self, queue, tiles):
        deps = []
        for t in tiles:
            if t.lw is not None:
                deps.append(t.lw)
            deps.extend(t.rd)
        self.ops[queue].append({"fn": None, "deps": list(dict.fromkeys(deps)), "ev": None, "kind": "w", "lazy": ()})

    def emit(self):
        nc = self.nc
        need = {e: set() for e in ENGS}
        for e in ENGS:
            for o in self.ops[e]:
                if o["lazy"]:
                    extra = [ev for t in o["lazy"] for ev in t.hist if ev != o["ev"]]
                    o["deps"] = list(dict.fromkeys(list(o["deps"]) + extra))
                for d in o["deps"]:
                    if d[0] == "eng":
                        need[d[1]].add(d[2])
        rank = {}
        for e in ENGS:
            r = 0
            for i, o in enumerate(self.ops[e]):
                if o["kind"] == "c" and i in need[e]:
                    r += 1
                    rank[(e, i)] = r
        with contextlib.ExitStack() as st:
            esem = {e: st.enter_context(nc.semaphore("s_" + e)) for e in ENGS}
            dsem = [st.enter_context(nc.semaphore("d%d" % i)) for i in range(self.n_dma_sem)]
            block = st.enter_context(nc.Block())

            def run(e, engobj):
                waited = {}
                for i, o in enumerate(self.ops[e]):
                    for d in o["deps"]:
                        if d[0] == "eng":
                            key, val, sem = ("e", d[1]), rank[(d[1], d[2])], esem[d[1]]
                        else:
                            key, val, sem = ("d", d[1]), d[2], dsem[d[1]]
                        if waited.get(key, 0) >= val:
                            continue
                        waited[key] = val
                        engobj.wait_ge(sem, val)
                    if o["fn"] is None:
                        continue
                    ins = o["fn"](engobj)
                    if o["kind"] == "d":
                        ins.then_inc(dsem[o["ev"][1]], 16)
                    elif (e, i) in rank:
                        ins.then_inc(esem[e], 1)

            block.tensor(lambda eng: run("pe", eng))
            block.scalar(lambda eng: run("act", eng))
            block.vector(lambda eng: run("dve", eng))
            block.gpsimd(lambda eng: run("pool", eng))
            block.sync(lambda eng: run("sp", eng))


def mk(name, *args, **kw):
    return lambda e: getattr(e, name)(*args, **kw)


def _tile_kxn(W):
    k, n = W.shape
    return np.ascontiguousarray(W.reshape(k // 128, 128, n).transpose(1, 0, 2)).reshape(128, (k // 128) * n)


def _rot_perm(cols):
    cols = np.asarray(cols)
    j = np.arange(cols.shape[0])
    return cols[(j // 32) * 32 + ((j % 32) + 16) % 32]


def _weight_plan():
    plan = []
    for h in range(4):
        plan.append(("attn_qk", h, 8 * 256))
        plan.append(("attn_v", h, 8 * 128))
    plan.append(("four", 0, 8 * 256))
    plan.append(("conv_cx", 0, 8 * 512))
    plan.append(("conv_b", 0, 8 * 256))
    for g in range(4):
        plan.append(("gate_fc", g, 8 * 512))
        plan.append(("gate_a_wo", g, 8 * 512))
        plan.append(("wout", g, 2 * 1024))
    for fg in range(6):
        nf = 4 if fg < 5 else 2
        plan.append(("ffn_g", fg, 8 * nf * 128))
        plan.append(("ffn_u", fg, 8 * nf * 128))
        plan.append(("ffn_d", fg, nf * 1024))
    return plan


def _pack_weights(inp, l):
    w_in = inp["w_in"][l]
    wcat = np.concatenate([inp["w_fourier_out"][l], inp["w_conv_out"][l], inp["w_attn_out"][l]], axis=0)
    parts = []
    for kind, a, free in _weight_plan():
        if kind == "attn_qk":
            q = Q_OFF + a * 128 + np.arange(128)
            k = K_OFF + a * 128 + np.arange(128)
            cols = np.concatenate([q, k])
            t = _tile_kxn(w_in[:, cols])
        elif kind == "attn_v":
            t = _tile_kxn(w_in[:, V_OFF + a * 128:V_OFF + (a + 1) * 128])
        elif kind == "four":
            t = _tile_kxn(w_in[:, F_OFF:F_OFF + 256])
        elif kind == "conv_cx":
            cols = np.concatenate([CC_OFF + np.arange(128), CX_OFF + np.arange(128),
                                   CC_OFF + 128 + np.arange(128), CX_OFF + 128 + np.arange(128)])
            t = _tile_kxn(w_in[:, cols])
        elif kind == "conv_b":
            t = _tile_kxn(w_in[:, CB_OFF:CB_OFF + 256])
        elif kind == "gate_fc":
            cols = np.concatenate([G_OFF + a * 256 + np.arange(256), G_OFF + 1024 + a * 256 + np.arange(256)])
            t = _tile_kxn(w_in[:, cols])
        elif kind == "gate_a_wo":
            t = np.concatenate([_tile_kxn(w_in[:, G_OFF + 2048 + a * 256:G_OFF + 2048 + (a + 1) * 256]),
                                _tile_kxn(wcat[:, a * 256:(a + 1) * 256])], axis=1)
        elif kind == "wout":
            t = _tile_kxn(inp["w_out"][l][a * 256:(a + 1) * 256, :])
        elif kind in ("ffn_g", "ffn_u"):
            nf = 4 if a < 5 else 2
            W = inp["w_ffn_gate" if kind == "ffn_g" else "w_ffn_up"][l]
            t = _tile_kxn(W[:, a * 512:a * 512 + nf * 128])
        elif kind == "ffn_d":
            nf = 4 if a < 5 else 2
            t = _tile_kxn(inp["w_ffn_down"][l][a * 512:a * 512 + nf * 128, :])
        assert t.shape == (128, free), (kind, t.shape, free)
        parts.append(t)
    return np.ascontiguousarray(np.concatenate(parts, axis=1), dtype=np.float32)


def _col(v):
    v = np.asarray(v)
    return np.ascontiguousarray(v.reshape(-1, 128).T)


def _pack_vecs(inp):
    cols = []
    for l in range(DEPTH):
        b_in = inp["b_in"][l]
        c = [_col(inp["b_ada"][l]), _col(inp["norm1_g"][l]), _col(inp["norm2_g"][l])]
        for h in range(4):
            q = Q_OFF + h * 128 + np.arange(128)
            k = K_OFF + h * 128 + np.arange(128)
            for cc in (q, _rot_perm(q), k, _rot_perm(k)):
                c.append(_col(b_in[cc]))
        c.append(_col(b_in[F_OFF:F_OFF + 256]))
        for cc in (CC_OFF, CX_OFF, CC_OFF + 128, CX_OFF + 128):
            c.append(_col(b_in[cc:cc + 128]))
        c.append(_col(b_in[CB_OFF:CB_OFF + 256]))
        c.append(_col(b_in[G_OFF:G_OFF + 3072]))
        for k in range(3):
            c.append(_col(inp["conv_w"][l][k]))
        c.append(_col(inp["conv_b"][l]))
        c.append(_col(inp["subln_g"][l]))
        lam = np.zeros((128, 4), np.float32)
        for i, nm in enumerate(("lambda_q1", "lambda_k1", "lambda_q2", "lambda_k2")):
            lam[:64, i] = inp[nm][l]
        c.append(lam)
        blk = np.concatenate(c, axis=1)
        assert blk.shape == (128, NV_L), blk.shape
        cols.append(blk)
    cols.append(_col(inp["final_g"]))
    return np.ascontiguousarray(np.concatenate(cols, axis=1), dtype=np.float32)


_CONST_CACHE = {}


def _const_tables():
    if _CONST_CACHE:
        return _CONST_CACHE
    t = np.arange(L)
    pos_r, pos_c = t // 64, t % 64
    freqs = 10000.0 ** (-(np.arange(0, 32, 2, dtype=np.float64)) / 32.0)
    rope = np.zeros((128, 2, L), np.float64)
    for p in range(128):
        j = p % 64
        pos = pos_r if j < 32 else pos_c
        i = j % 16
        half = (j % 32) // 16
        ang = pos.astype(np.float32).astype(np.float64) * np.float32(freqs[i]).astype(np.float64)
        rope[p, 0] = np.cos(ang)
        rope[p, 1] = np.sin(ang) * (-1.0 if half == 0 else 1.0)
    _CONST_CACHE["rope"] = rope.astype(np.float32)
    tt = np.arange(L, dtype=np.float64)
    ang = 2.0 * np.pi * ((tt[:, None] * tt[None, :]) % L) / L
    CL = np.cos(ang) / math.sqrt(L)
    SL = -np.sin(ang) / math.sqrt(L)
    tab = np.stack([CL, SL], axis=0)
    tab = tab.reshape(2, 8, 2, 128, 4, 512)
    tab = tab.transpose(4, 1, 3, 0, 2, 5)
    _CONST_CACHE["dftL"] = np.ascontiguousarray(tab).reshape(4, 8, 128, 2048).astype(ml_dtypes.bfloat16)
    tc = np.arange(CTX, dtype=np.float64)
    angc = 2.0 * np.pi * ((tc[:, None] * tc[None, :]) % CTX) / CTX
    tabc = np.stack([np.cos(angc), -np.sin(angc)], axis=0) / math.sqrt(CTX)
    tabc = tabc.reshape(2, 2, 128, CTX).transpose(2, 0, 1, 3)
    _CONST_CACHE["dftC"] = np.ascontiguousarray(tabc).reshape(128, 1024).astype(ml_dtypes.bfloat16)
    c = np.arange(128)
    same = (c[:, None] // 64) == (c[None, :] // 64)
    angh = 2.0 * np.pi * (((c[:, None] % 64) * (c[None, :] % 64)) % 64) / 64.0
    ch = np.concatenate([np.where(same, np.cos(angh), 0.0), np.where(same, np.sin(angh), 0.0)], axis=1) / 8.0
    _CONST_CACHE["dftch"] = np.ascontiguousarray(ch).astype(ml_dtypes.bfloat16)
    pm = np.zeros((128, 128), np.float32)
    pm[_rot_perm(np.arange(128)), np.arange(128)] = 1.0
    _CONST_CACHE["permm"] = pm.astype(ml_dtypes.bfloat16)
    return _CONST_CACHE


def build_program(debug=None, stop_after=None, n_layers=DEPTH):
    debug = debug or []
    nc = bass.Bass("TRN2", target_bir_lowering=False)
    plan = _weight_plan()
    TOT = sum(f for _, _, f in plan)
    xt_d = nc.dram_tensor("xt", [D, NTOK], F32, kind="ExternalInput").ap()
    cv_d = nc.dram_tensor("cv", [128, 16], F32, kind="ExternalInput").ap()
    wada_d = nc.dram_tensor("wada", [DEPTH, 48, 128, 1024], F32, kind="ExternalInput").ap()
    wb_d = nc.dram_tensor("wb", [DEPTH, 128, TOT], F32, kind="ExternalInput").ap()
    vecs_d = nc.dram_tensor("vecs", [128, DEPTH * NV_L + 8], F32, kind="ExternalInput").ap()
    bvb_d = nc.dram_tensor("bvb", [DEPTH, 128, 512], F32, kind="ExternalInput").ap()
    rope_d = nc.dram_tensor("rope", [128, 2 * L], F32, kind="ExternalInput").ap()
    dftL_d = nc.dram_tensor("dftL", [4, 8, 128, 2048], BF16, kind="ExternalInput").ap()
    dftC_d = nc.dram_tensor("dftC", [128, 1024], BF16, kind="ExternalInput").ap()
    dftch_d = nc.dram_tensor("dftch", [128, 256], BF16, kind="ExternalInput").ap()
    permm_d = nc.dram_tensor("permm", [128, 128], BF16, kind="ExternalInput").ap()
    out_d = nc.dram_tensor("outT", [D, L], F32, kind="ExternalOutput").ap()
    dbg_out = {}

    P = Prog(nc)
    es = contextlib.ExitStack()
    with es:
        ARENA_F32 = 53200
        arena = es.enter_context(nc.sbuf_tensor("arena", [128, ARENA_F32], F32))
        psum_all = es.enter_context(nc.psum_tensor("psall", [128, 4096], F32))
        PS = [Tile("ps%d" % i) for i in range(8)]
        ps = [psum_all[:, i * 512:(i + 1) * 512] for i in range(8)]

        def view(off, nelem, dt):
            if dt == F32:
                assert off % 4 == 0
                return arena[:, off // 4: off // 4 + nelem]
            assert off % 4 == 0 and nelem % 2 == 0
            return arena[:, off // 4: off // 4 + nelem // 2].bitcast(BF16)

        O_X = 0
        O_H = O_X + 8 * NTOK * 4
        O_BR = O_H + 8 * NTOK * 2
        O_RING = O_BR + 8 * NTOK * 2
        O_CONST = O_RING + 3 * 8192
        O_PH = O_CONST + 8192
        PH_SIZE = ARENA_F32 * 4 - O_PH
        xT = view(O_X, 8 * NTOK, F32).rearrange("p (k t) -> p k t", k=8)
        hT = view(O_H, 8 * NTOK, BF16).rearrange("p (k t) -> p k t", k=8)
        oT = view(O_BR, 4 * NTOK, BF16).rearrange("p (k t) -> p k t", k=4)
        yfT = view(O_BR + 4 * NTOK * 2, 2 * NTOK, BF16).rearrange("p (k t) -> p k t", k=2)
        ycT = view(O_BR + 6 * NTOK * 2, 2 * NTOK, BF16).rearrange("p (k t) -> p k t", k=2)
        ropeT = view(O_BR + 4 * NTOK * 2, 2 * L, F32).rearrange("p (k t) -> p k t", k=2)
        ring = [view(O_RING + i * 8192, 4096, BF16) for i in range(3)]
        T_x = [[Tile("x%d_%d" % (k, b)) for b in range(5)] for k in range(8)]
        T_h = [[Tile("h%d_%d" % (k, b)) for b in range(5)] for k in range(8)]
        T_o = [[Tile("o%d_%d" % (k, b)) for b in range(5)] for k in range(4)]
        T_yf = [[Tile("yf%d_%d" % (k, b)) for b in range(5)] for k in range(2)]
        T_yc = [[Tile("yc%d_%d" % (k, b)) for b in range(5)] for k in range(2)]
        T_rope = Tile("rope")
        T_ring = [Tile("ring%d" % i) for i in range(3)]

        co = [O_CONST]

        def calloc(nbytes):
            o = co[0]
            co[0] += (nbytes + 3) // 4 * 4
            assert co[0] <= O_PH
            return o
        NV = DEPTH * NV_L + 8
        vecs = view(calloc(NV * 4), NV, F32)
        modv = view(calloc(DEPTH * 96 * 4), DEPTH * 96, F32).rearrange("p (l c m) -> p l c m", l=DEPTH, m=2)
        derived = view(calloc(DEPTH * 64 * 4), DEPTH * 64, F32).rearrange("p (l j k m) -> p l j k m", l=DEPTH, j=4, m=2)
        sv = view(calloc(64), 16, F32).rearrange("p (k m) -> p k m", m=2)
        cvs = view(calloc(64), 16, F32)
        svb = view(calloc(32), 16, BF16).rearrange("p (k m) -> p k m", m=2)
        lamv = view(calloc(DEPTH * 16), DEPTH * 4, F32).rearrange("p (l k) -> p l k", l=DEPTH)
        gsub = view(calloc(DEPTH * 4), DEPTH, F32)
        ones_bf = view(calloc(256), 128, BF16)
        ones_f = view(calloc(512), 128, F32)
        dftch = view(calloc(512), 256, BF16)
        dftC = view(calloc(2048), 1024, BF16).rearrange("p (c t n) -> p c t n", c=2, t=2)
        epsv = view(calloc(4), 1, F32)
        permM = view(calloc(256), 128, BF16)
        T_const = Tile("const")
        T_modv = Tile("modv")

        def vcol(l, j):
            return vecs[:, l * NV_L + j: l * NV_L + j + 1]

        def tap(name, ap, tiles, shape, dt):
            if name not in debug:
                return
            d = nc.dram_tensor("dbg_" + name, shape, dt, kind="ExternalOutput").ap()
            dbg_out[name] = Tile("dbg_" + name)
            P.dma("sp", mk("dma_start", out=d, in_=ap), reads=tiles, writes=[dbg_out[name]])

        wseq = []
        for l in range(n_layers):
            off = 0
            for kind, a, free in plan:
                wseq.append((l, off, free, kind, a))
                off += free
        wstate = {"used": 0, "issued": 0}
        T_wgen = [Tile("wgen%d" % i) for i in range(len(wseq))]

        def w_issue_upto(n):
            while wstate["issued"] < min(n, len(wseq)):
                i = wstate["issued"]
                l_, off_, free_, _, _ = wseq[i]
                s_ = i % 3
                P.dma("pool", mk("dma_start",
                    out=ring[s_][:, 0:free_], in_=wb_d[l_, :, off_:off_ + free_], max_dma_last_dim=8192),
                    writes=[T_wgen[i]], sem_tile=T_ring[s_], lazy=([T_wgen[i - 3]] if i >= 3 else []))
                wstate["issued"] += 1

        def w_next(kind, a):
            i = wstate["used"]
            l, off, free, k2, a2 = wseq[i]
            assert (k2, a2) == (kind, a), (kind, a, k2, a2)
            w_issue_upto(i + 3)
            wstate["used"] += 1
            return ring[i % 3], T_wgen[i]

        ph = {"off": 0, "tiles": [], "prev": []}

        def ph_reset():
            ph["prev"] = ph["prev"] + ph["tiles"]
            ph["tiles"] = []
            ph["off"] = 0

        def ph_alloc(name, nelem, dt):
            nb = nelem * (4 if dt == F32 else 2)
            nb = (nb + 3) // 4 * 4
            assert ph["off"] + nb <= PH_SIZE, ("phase region overflow", name, ph["off"], nb)
            ap = view(O_PH + ph["off"], nelem, dt)
            ph["off"] += nb
            t = Tile(name)
            P.alias([t], ph["prev"])
            ph["tiles"].append(t)
            return ap, t

        def ph_tiles(tl):
            P.alias(tl, ph["prev"])
            ph["tiles"].extend(tl)

        def ph_commit():
            ph["prev"] = []

        P.dma("sp", mk("dma_start", out=vecs, in_=vecs_d), writes=[T_const])
        P.dma("sp", mk("dma_start", out=cvs, in_=cv_d), writes=[T_const])
        P.dma("sp", mk("dma_start", out=dftch, in_=dftch_d), writes=[T_const])
        P.dma("sp", mk("dma_start", out=permM, in_=permm_d), writes=[T_const])
        P.dma("sp", mk("dma_start", out=dftC.rearrange("p c t n -> p (c t n)"), in_=dftC_d), writes=[T_const])
        P.op("dve", mk("memset", ones_bf, 1.0), writes=[T_const])
        P.op("dve", mk("memset", ones_f, 1.0), writes=[T_const])
        P.op("dve", mk("memset", epsv, EPS), writes=[T_const])
        for b, (t0, tn) in enumerate(TB):
            P.dma("sp", mk("dma_start",
                out=xT[:, :, t0:t0 + tn], in_=xt_d.rearrange("(k p) t -> p k t", p=128)[:, :, t0:t0 + tn]),
                writes=[T_x[k][b] for k in range(8)], sem_tile=T_x[0][b])
        P.op("act", mk("activation", out=svb.rearrange("p k m -> p (k m)"), in_=cvs, func=AF.Silu),
             reads=[T_const], writes=[T_modv])
        T_mod = [Tile("mod%d" % l) for l in range(DEPTH)]

        def adaln_cols(l, cols, wst, cnt, bank, tile=None):
            for col in cols:
                wap, wt = wst[cnt[0] % len(wst)]
                cnt[0] += 1
                P.dma("pool", mk("dma_start", out=wap, in_=wada_d[l, col], max_dma_last_dim=8192), writes=[wt])
                w3 = wap.rearrange("p (k n) -> p k n", k=8)
                for kc in range(8):
                    P.op("pe", mk("matmul", out=ps[bank][:, 0:2], lhsT=w3[:, kc, :], rhs=svb[:, kc, :],
                                  start=(kc == 0), stop=(kc == 7)), reads=[wt, T_modv], writes=[PS[bank]])
                P.op("dve", mk("tensor_scalar", out=modv[:, l, col, :], in0=ps[bank][:, 0:2],
                               scalar1=vecs[:, l * NV_L + col: l * NV_L + col + 1], scalar2=None, op0=ALU.add),
                     reads=[PS[bank], T_const], writes=[tile or T_mod[l]])

        ada_state = {"next": 0}

        def ada_begin(l, slots):
            pairs = []
            for (wap, wt) in slots:
                col = ada_state["next"]
                if col >= 48:
                    break
                ada_state["next"] += 1
                P.dma("pool", mk("dma_start", out=wap, in_=wada_d[l, col], max_dma_last_dim=8192), writes=[wt])
                pairs.append((col, wap, wt))
            return pairs

        def ada_end(l, pairs, bank):
            for (col, wap, wt) in pairs:
                w3 = wap.rearrange("p (k n) -> p k n", k=8)
                for kc in range(8):
                    P.op("pe", mk("matmul", out=ps[bank][:, 0:2], lhsT=w3[:, kc, :], rhs=svb[:, kc, :],
                                  start=(kc == 0), stop=(kc == 7)), reads=[wt, T_modv], writes=[PS[bank]])
                P.op("dve", mk("tensor_scalar", out=modv[:, l, col, :], in0=ps[bank][:, 0:2],
                               scalar1=vecs[:, l * NV_L + col: l * NV_L + col + 1], scalar2=None, op0=ALU.add),
                     reads=[PS[bank], T_const], writes=[T_mod[l]])

        def adaln_finish_a(l, tile):
            for m in range(2):
                P.op("dve", mk("scalar_tensor_tensor",
                    out=derived[:, l, 0, :, m], in0=modv[:, l, 8:16, m], scalar=1.0,
                    in1=vecs[:, l * NV_L + 48: l * NV_L + 56], op0=ALU.add, op1=ALU.mult),
                    reads=[tile, T_const], writes=[tile])

        def adaln_finish(l):
            adaln_finish_a(l, T_mod[l])
            adaln_finish_b(l)

        def adaln_finish_b(l):
            for m in range(2):
                P.op("dve", mk("scalar_tensor_tensor",
                    out=derived[:, l, 1, :, m], in0=modv[:, l, 32:40, m], scalar=1.0,
                    in1=vecs[:, l * NV_L + 56: l * NV_L + 64], op0=ALU.add, op1=ALU.mult),
                    reads=[T_mod[l], T_const], writes=[T_mod[l]])
            lam_init = 0.8 - 0.6 * math.exp(-0.3 * l)
            P.op("dve", mk("tensor_tensor", out=lamv[:, l, 0:1], in0=vcol(l, 121), in1=vcol(l, 122), op=ALU.mult),
                 reads=[T_const], writes=[T_mod[l]])
            P.op("dve", mk("tensor_tensor", out=lamv[:, l, 1:2], in0=vcol(l, 123), in1=vcol(l, 124), op=ALU.mult),
                 reads=[T_const], writes=[T_mod[l]])
            P.op("pe", mk("matmul", out=ps[1][:, 0:2], lhsT=ones_f, rhs=lamv[:, l, 0:2], start=True, stop=True),
                 reads=[T_mod[l], T_const], writes=[PS[1]])
            P.op("act", mk("activation", out=lamv[:, l, 2:4], in_=ps[1][:, 0:2], func=AF.Exp),
                 reads=[PS[1]], writes=[T_mod[l]])
            P.op("dve", mk("scalar_tensor_tensor",
                out=lamv[:, l, 0:1], in0=lamv[:, l, 3:4], scalar=-lam_init, in1=lamv[:, l, 2:3],
                op0=ALU.add, op1=ALU.subtract), reads=[T_mod[l]], writes=[T_mod[l]])
            P.op("dve", mk("tensor_scalar",
                out=gsub[:, l:l + 1], in0=vcol(l, 120), scalar1=(1.0 - lam_init), scalar2=None, op0=ALU.mult),
                reads=[T_const], writes=[T_mod[l]])

        prologue_n1 = {}

        def prologue():
            T_modA = Tile("modA")
            ph_reset()
            wst0 = [ph_alloc("wada%d" % i, 1024, BF16) for i in range(8)]
            sq, _ = ph_alloc("sq", 8 * 512, BF16)
            nset["sq"] = sq.rearrange("p (k t) -> p k t", k=8)
            nset["t_sq"] = [Tile("sq%d" % k) for k in range(8)]
            ph_tiles(nset["t_sq"])
            nset["rstd"], nset["t_rstd"] = ph_alloc("rstd", 512, F32)
            nset["tmp"] = [ph_alloc("ntmp%d" % i, 512, F32) for i in range(2)]
            ph_commit()
            cnt0 = [0]
            adaln_cols(0, range(16), wst0, cnt0, 0, tile=T_modA)
            adaln_finish_a(0, T_modA)
            rest = list(range(16, 48))

            def between():
                cols = rest[:8]
                del rest[:8]
                adaln_cols(0, cols, wst0, cnt0, 0)
            norm_phase(0, 0, [0, 1, 2, 3, 4], reuse=True, between=between, mod_tile=T_modA)
            adaln_cols(0, list(rest), wst0, cnt0, 0)
            adaln_finish_b(0)
        prologue_n1["fn"] = prologue
        tap("modv", modv.rearrange("p l c m -> p (l c m)"), [T_mod[0]], [128, DEPTH * 96], F32)
        tap("lamv", lamv.rearrange("p l k -> p (l k)"), [T_mod[0]], [128, DEPTH * 4], F32)

        def A_col(l, which, kc, m):
            return derived[:, l, which, kc, m:m + 1]

        def mod_col(l, j, kc, m):
            return modv[:, l, j * 8 + kc, m:m + 1]

        def rms_rstd(src_tiles, src_ap_fn, nk, width, inv_n, sq, t_sq, rstd, t_rstd, bank):
            for kc in range(nk):
                P.op("act", mk("activation", out=sq[:, kc, 0:width], in_=src_ap_fn(kc), func=AF.Square),
                     reads=[src_tiles[kc]], writes=[t_sq[kc]])
            for kc in range(nk):
                P.op("pe", mk("matmul", out=ps[bank][:, 0:width], lhsT=ones_bf, rhs=sq[:, kc, 0:width],
                                                      start=(kc == 0), stop=(kc == nk - 1)),
                     reads=[t_sq[kc], T_const], writes=[PS[bank]])
            P.op("act", mk("activation", out=rstd[:, 0:width], in_=ps[bank][:, 0:width], func=AF.Ln,
                                               scale=inv_n, bias=epsv),
                 reads=[PS[bank], T_const], writes=[t_rstd])
            P.op("act", mk("activation", out=rstd[:, 0:width], in_=rstd[:, 0:width], func=AF.Exp, scale=-0.5),
                 reads=[t_rstd], writes=[t_rstd])

        nset = {}

        def norm_phase(l, which, blocks, reuse=False, commit=True, between=None, mod_tile=None):
            if not (reuse and nset):
                ph_reset()
                sq, _ = ph_alloc("sq", 8 * 512, BF16)
                nset["sq"] = sq.rearrange("p (k t) -> p k t", k=8)
                nset["t_sq"] = [Tile("sq%d" % k) for k in range(8)]
                ph_tiles(nset["t_sq"])
                nset["rstd"], nset["t_rstd"] = ph_alloc("rstd", 512, F32)
                nset["tmp"] = [ph_alloc("ntmp%d" % i, 512, F32) for i in range(2)]
                if commit:
                    ph_commit()
            sq, t_sq, rstd, t_rstd, tmp = nset["sq"], nset["t_sq"], nset["rstd"], nset["t_rstd"], nset["tmp"]
            mt_ = mod_tile or T_mod[l]
            c = 0
            for bi_, b in enumerate(blocks):
                if between is not None and bi_ > 0:
                    between()
                t0, tn = TB[b]
                m = 1 if b == 4 else 0
                rms_rstd([T_x[k][b] for k in range(8)], lambda kc, t0=t0, tn=tn: xT[:, kc, t0:t0 + tn], 8, tn,
                         1.0 / D, sq, t_sq, rstd, t_rstd, 7)
                for kc in range(8):
                    tap_, tt_ = tmp[c % 2]
                    c += 1
                    P.op("dve", mk("scalar_tensor_tensor",
                        out=tap_[:, 0:tn], in0=xT[:, kc, t0:t0 + tn], scalar=A_col(l, which, kc, m),
                        in1=rstd[:, 0:tn], op0=ALU.mult, op1=ALU.mult),
                        reads=[T_x[kc][b], t_rstd, mt_], writes=[tt_])
                    P.op("act", mk("activation",
                        out=hT[:, kc, t0:t0 + tn], in_=tap_[:, 0:tn], func=AF.Identity,
                        bias=mod_col(l, 0 if which == 0 else 3, kc, m), scale=1.0),
                        reads=[tt_, mt_], writes=[T_h[kc][b]])

        for l in range(n_layers):
            last = (l == DEPTH - 1)
            qblocks = [0, 1, 2, 3] if last else [0, 1, 2, 3, 4]
            vb = l * NV_L
            if l == 0:
                prologue_n1["fn"]()
            else:
                norm_phase(l, 0, [0, 1, 2, 3, 4], reuse=True)
            if l == 0:
                tap("h0", hT.rearrange("p k t -> p (k t)"), [T_h[k][b] for k in range(8) for b in range(5)], [128, 8 * NTOK], BF16)
            if stop_after == "n1":
                break
            ph_reset()
            qh, _ = ph_alloc("qh", NTOK, BF16)
            kh, _ = ph_alloc("kh", NTOK, BF16)
            vh, _ = ph_alloc("vh", 18 * 128, BF16)
            vh = vh.rearrange("p (t n) -> p t n", t=18)
            T_q = [Tile("q%d" % b) for b in range(5)]
            T_k = [Tile("k%d" % b) for b in range(5)]
            T_v = [Tile("v%d" % t) for t in range(18)]
            ph_tiles(T_q + T_k + T_v)
            pbuf = [ph_alloc("p%d" % i, 1024, BF16) for i in range(3)]
            r1, t_r1 = ph_alloc("r1", 512, F32)
            r2, t_r2 = ph_alloc("r2", 512, F32)
            ob, t_ob = ph_alloc("ob", 512, F32)
            osq, t_osq = ph_alloc("osq", 512, BF16)
            bvt, t_bvt = ph_alloc("bvt", 128, F32)
            accs = [ph_alloc("acc%d" % i, 512, F32) for i in range(2)]
            ph_commit()
            P.alias([T_rope], [T_yf[k][b] for k in range(2) for b in range(5)] + [T_yc[k][b] for k in range(2) for b in range(5)])
            t_p3 = Tile("p3")
            P.alias([t_p3], [T_yf[k][b] for k in range(2) for b in range(5)] + [T_yc[k][b] for k in range(2) for b in range(5)])
            pbuf = pbuf + [(view(O_BR + 4 * NTOK * 2 + 2 * L * 4, 1024, BF16), t_p3)]
            P.dma("sp", mk("dma_start", out=ropeT.rearrange("p k t -> p (k t)"), in_=rope_d), writes=[T_rope])
            pcount = 0
            for h in range(4):
                wqk, t_wqk = w_next("attn_qk", h)
                wqk3 = wqk[:, 0:2048].rearrange("p (k n) -> p k n", k=8)
                bq = [vcol(l, 64 + h * 4 + i) for i in range(4)]
                ktmp = accs[1][0].bitcast(BF16)
                t_ktmp = accs[1][1]
                for b in range(5):
                    t0, tn = TB[b]
                    need_q = b in qblocks
                    base = 0 if b % 2 == 0 else 4
                    todo = [(dst, T_d, base + c0, c0, wc, tmp_, t_tmp_) for (dst, T_d, c0, wc, tmp_, t_tmp_) in
                            ((qh, T_q, 0, 0, osq, t_osq), (kh, T_k, 2, 1, ktmp, t_ktmp)) if (c0 != 0 or need_q)]
                    for (dst, T_d, bk, c0, wc, tmp_, t_tmp_) in todo:
                        for kc in range(8):
                            P.op("pe", mk("matmul",
                                out=ps[bk][:, 0:tn], lhsT=wqk3[:, kc, wc * 128:(wc + 1) * 128],
                                rhs=hT[:, kc, t0:t0 + tn], start=(kc == 0), stop=(kc == 7)),
                                reads=[t_wqk, T_h[kc][b]], writes=[PS[bk]])
                        if b < 4:
                            P.op("act", mk("activation",
                                out=tmp_[:, 0:tn], in_=ps[bk][:, 0:tn], func=AF.Identity, bias=bq[c0], scale=1.0),
                                reads=[PS[bk], T_const], writes=[t_tmp_])
                        else:
                            P.op("act", mk("activation",
                                out=dst[:, t0:t0 + tn], in_=ps[bk][:, 0:tn], func=AF.Identity, bias=bq[c0], scale=1.0),
                                reads=[PS[bk], T_const], writes=[T_d[b]])
                    if b < 4:
                        for (dst, T_d, bk, c0, wc, tmp_, t_tmp_) in todo:
                            P.op("pe", mk("matmul", out=ps[bk + 1][:, 0:tn], lhsT=permM, rhs=tmp_[:, 0:tn], start=True, stop=True),
                                 reads=[t_tmp_, T_const], writes=[PS[bk + 1]])
                        for (dst, T_d, bk, c0, wc, tmp_, t_tmp_) in todo:
                            P.op("dve", mk("scalar_tensor_tensor",
                                out=r1[:, 0:tn], in0=ps[bk][:, 0:tn], scalar=bq[c0], in1=ropeT[:, 0, t0:t0 + tn],
                                op0=ALU.add, op1=ALU.mult), reads=[PS[bk], T_rope, T_const, t_tmp_], writes=[t_r1])
                            P.op("dve", mk("tensor_tensor",
                                out=r2[:, 0:tn], in0=ps[bk + 1][:, 0:tn], in1=ropeT[:, 1, t0:t0 + tn], op=ALU.mult),
                                reads=[PS[bk + 1], T_rope], writes=[t_r2])
                            P.op("dve", mk("tensor_tensor",
                                out=dst[:, t0:t0 + tn], in0=r1[:, 0:tn], in1=r2[:, 0:tn], op=ALU.add),
                                reads=[t_r1, t_r2], writes=[T_d[b]])
                P.dma("sp", mk("dma_start", out=bvt, in_=bvb_d[l][:, h * 128:(h + 1) * 128]), writes=[t_bvt])
                wv, t_wv = w_next("attn_v", h)
                wv3 = wv[:, 0:1024].rearrange("p (k n) -> p k n", k=8)
                for tt in range(18):
                    b = min(tt // 4, 4)
                    bank = 4 + (tt % 2)
                    for kc in range(8):
                        P.op("pe", mk("matmul",
                            out=ps[bank][:, 0:128], lhsT=hT[:, kc, tt * 128:(tt + 1) * 128], rhs=wv3[:, kc, :],
                            start=(kc == 0), stop=(kc == 7)), reads=[t_wv, T_h[kc][b]], writes=[PS[bank]])
                    P.op("dve", mk("tensor_tensor",
                        out=vh[:, tt, :], in0=ps[bank][:, 0:128], in1=bvt[:, 0:128], op=ALU.add),
                        reads=[PS[bank], t_bvt], writes=[T_v[tt]])
                if l == 0:
                    tap("q%d" % h, qh, T_q, [128, NTOK], BF16)
                    tap("k%d" % h, kh, T_k, [128, NTOK], BF16)
                    tap("v%d" % h, vh.rearrange("p t n -> p (t n)"), T_v, [128, 18 * 128], BF16)
                steps = []
                for qb in qblocks:
                    kts = list(range(18)) if qb < 4 else [16, 17]
                    for i, kt in enumerate(kts):
                        steps.append((qb, i, kt, len(kts)))

                def emit_qk(si):
                    qb, i, kt, n = steps[si]
                    q0, qn = TB[qb]
                    kb = min(kt // 4, 4)
                    sb = (si % 2) * 2
                    for c in range(2):
                        P.op("pe", mk("matmul",
                            out=ps[sb + c][:, 0:qn], lhsT=kh[c * 64:(c + 1) * 64, kt * 128:(kt + 1) * 128],
                            rhs=qh[c * 64:(c + 1) * 64, q0:q0 + qn], start=True, stop=True),
                            reads=[T_k[kb], T_q[qb]], writes=[PS[sb + c]])

                pending = []
                pending1 = []
                acc0, t_acc0 = accs[0]
                csum3 = accs[1][0].bitcast(BF16).rearrange("p (c n) -> p c n", c=2)
                t_csum = accs[1][1]
                prev_p = None
                emit_qk(0)
                if len(steps) > 1:
                    emit_qk(1)
                for si, (qb, i, kt, nkt) in enumerate(steps):
                    q0, qn = TB[qb]
                    sb = (si % 2) * 2
                    pa2, pt = pbuf[pcount % 4]
                    pcount += 1
                    pa3 = pa2.rearrange("p (c n) -> p c n", c=2)
                    P.op("act", mk("activation",
                        out=pa3[:, :, 0:qn],
                        in_=psum_all[:, sb * 512:(sb + 2) * 512].rearrange("p (c n) -> p c n", c=2)[:, :, 0:qn],
                        func=AF.Exp, scale=0.125), reads=[PS[sb], PS[sb + 1]], writes=[pt])
                    if si + 2 < len(steps):
                        emit_qk(si + 2)
                    for c in range(2):
                        P.op("pe", mk("matmul",
                            out=ps[4 + c][:, 0:qn], lhsT=vh[:, kt, :], rhs=pa3[:, c, 0:qn],
                            start=(i == 0), stop=(i == nkt - 1)), reads=[T_v[kt], pt], writes=[PS[4 + c]])
                    if i == 1:
                        P.op("dve", mk("tensor_tensor", out=csum3[:, :, 0:qn], in0=prev_p[0][:, :, 0:qn], in1=pa3[:, :, 0:qn], op=ALU.add),
                             reads=[prev_p[1], pt], writes=[t_csum])
                    elif i > 1:
                        P.op("dve", mk("tensor_tensor", out=csum3[:, :, 0:qn], in0=csum3[:, :, 0:qn], in1=pa3[:, :, 0:qn], op=ALU.add),
                             reads=[t_csum, pt], writes=[t_csum])
                    prev_p = (pa3, pt)
                    if pending1 and (i == min(1, nkt - 1)):
                        pending1.pop()()
                    if pending and (i == min(3, nkt - 1)):
                        pending.pop()(6)
                    if i == nkt - 1:
                        for c in range(2):
                            P.op("pe", mk("matmul", out=ps[6 + c][:, 0:qn], lhsT=ones_bf, rhs=csum3[:, c, 0:qn], start=True, stop=True),
                                 reads=[T_const, t_csum], writes=[PS[6 + c]])
                    if i != nkt - 1:
                        continue
                    P.op("dve", mk("tensor_copy", out=r1[:, 0:qn], in_=ps[4][:, 0:qn]), reads=[PS[4]], writes=[t_r1])
                    P.op("dve", mk("tensor_copy", out=r2[:, 0:qn], in_=ps[5][:, 0:qn]), reads=[PS[5]], writes=[t_r2])

                    def part1b(qn=qn, q0=q0, h=h, qb=qb):
                        P.op("act", mk("activation", out=acc0[:, 0:qn], in_=ps[6][:, 0:qn], func=AF.Ln), reads=[PS[6]], writes=[t_acc0])
                        P.op("act", mk("activation", out=ob[:, 0:qn], in_=ps[7][:, 0:qn], func=AF.Ln), reads=[PS[7]], writes=[t_ob])
                        P.op("act", mk("activation", out=acc0[:, 0:qn], in_=acc0[:, 0:qn], func=AF.Exp, scale=-1.0), reads=[t_acc0], writes=[t_acc0])
                        P.op("act", mk("activation", out=ob[:, 0:qn], in_=ob[:, 0:qn], func=AF.Exp, scale=-1.0), reads=[t_ob], writes=[t_ob])
                        P.op("dve", mk("tensor_tensor", out=r1[:, 0:qn], in0=r1[:, 0:qn], in1=acc0[:, 0:qn], op=ALU.mult),
                             reads=[t_r1, t_acc0], writes=[t_r1])
                        P.op("dve", mk("tensor_tensor", out=r2[:, 0:qn], in0=r2[:, 0:qn], in1=ob[:, 0:qn], op=ALU.mult),
                             reads=[t_r2, t_ob], writes=[t_r2])
                        P.op("dve", mk("scalar_tensor_tensor",
                            out=ob[:, 0:qn], in0=r2[:, 0:qn], scalar=lamv[:, l, 0:1], in1=r1[:, 0:qn],
                            op0=ALU.mult, op1=ALU.add), reads=[t_r1, t_r2, T_mod[l]], writes=[t_ob])
                        P.op("dve", mk("tensor_tensor", out=osq[:, 0:qn], in0=ob[:, 0:qn], in1=ob[:, 0:qn], op=ALU.mult),
                             reads=[t_ob], writes=[t_osq])

                        def part2(bank):
                            P.op("pe", mk("matmul", out=ps[bank][:, 0:qn], lhsT=ones_bf, rhs=osq[:, 0:qn], start=True, stop=True),
                                 reads=[t_osq, T_const], writes=[PS[bank]])
                            P.op("act", mk("activation", out=acc0[:, 0:qn], in_=ps[bank][:, 0:qn], func=AF.Ln,
                                           scale=1.0 / 128.0, bias=epsv), reads=[PS[bank], T_const], writes=[t_acc0])
                            P.op("act", mk("activation", out=acc0[:, 0:qn], in_=acc0[:, 0:qn], func=AF.Exp, scale=-0.5),
                                 reads=[t_acc0], writes=[t_acc0])
                            P.op("dve", mk("scalar_tensor_tensor",
                                out=oT[:, h, q0:q0 + qn], in0=ob[:, 0:qn], scalar=gsub[:, l:l + 1], in1=acc0[:, 0:qn],
                                op0=ALU.mult, op1=ALU.mult), reads=[t_ob, t_acc0, T_mod[l]], writes=[T_o[h][qb]])
                        pending.append(part2)
                    pending1.append(part1b)
                if pending1:
                    pending1.pop()()
                if pending:
                    pending.pop()(6)
            if l == 0:
                tap("o0", oT.rearrange("p k t -> p (k t)"), [T_o[k][b] for k in range(4) for b in range(5)], [128, 4 * NTOK], BF16)
            if stop_after == "attn":
                break
            ph_reset()
            wf, t_wf = w_next("four", 0)
            wf3 = wf[:, 0:2048].rearrange("p (k n) -> p k n", k=8)
            uT, _ = ph_alloc("uT", NTOK, BF16)
            T_u = [Tile("u%d" % b) for b in range(5)]
            ucs, _ = ph_alloc("ucs", 18 * 512, BF16)
            ucs = ucs.rearrange("p (t c s n) -> p t c s n", t=18, c=2, s=2)
            T_ucs = [[Tile("ucs%d_%d" % (t, c)) for c in range(2)] for t in range(18)]
            ph_tiles(T_u + [x for y in T_ucs for x in y])
            dring = [ph_alloc("dring%d" % i, 2048, BF16) for i in range(2)]
            ph_commit()
            P.alias([T_yf[k][b] for k in range(2) for b in range(5)] + [T_yc[k][b] for k in range(2) for b in range(5)], [T_rope, t_p3])
            ntt = 18 if not last else 16
            dcount = 0
            for cc in range(2):
                for b in qblocks:
                    t0, tn = TB[b]
                    for kc in range(8):
                        P.op("pe", mk("matmul",
                            out=ps[0][:, 0:tn], lhsT=wf3[:, kc, cc * 128:(cc + 1) * 128], rhs=hT[:, kc, t0:t0 + tn],
                            start=(kc == 0), stop=(kc == 7)), reads=[t_wf, T_h[kc][b]], writes=[PS[0]])
                    P.op("act", mk("activation",
                        out=uT[:, t0:t0 + tn], in_=ps[0][:, 0:tn], func=AF.Identity, bias=vcol(l, 80 + cc), scale=1.0),
                        reads=[PS[0], T_const], writes=[T_u[b]])
                for tt in range(ntt):
                    b = min(tt // 4, 4)
                    bank = 1 + (tt % 2)
                    P.op("pe", mk("matmul",
                        out=ps[bank][:, 0:256], lhsT=uT[:, tt * 128:(tt + 1) * 128], rhs=dftch, start=True, stop=True),
                        reads=[T_u[b], T_const], writes=[PS[bank]])
                    P.op("dve", mk("tensor_copy",
                        out=ucs[:, tt, cc, :, :].rearrange("p s n -> p (s n)"), in_=ps[bank][:, 0:256]),
                        reads=[PS[bank]], writes=[T_ucs[tt][cc]])
            for j in range(4):
                for piece in range(8):
                    da, dt_ = dring[dcount % 2]
                    dcount += 1
                    P.dma("sp", mk("dma_start", out=da, in_=dftL_d[j, piece]), writes=[dt_])
                    d4 = da.rearrange("p (c t n) -> p c t n", c=2, t=2)
                    for tl in range(2):
                        tt = piece * 2 + tl
                        for cc in range(2):
                            for cs in range(2):
                                P.op("pe", mk("matmul",
                                    out=ps[3 + cc][:, :], lhsT=ucs[:, tt, cc, cs, :], rhs=d4[:, cs, tl, :],
                                    start=(tt == 0 and cs == 0), stop=(tt == 15 and cs == 1)),
                                    reads=[T_ucs[tt][cc], dt_], writes=[PS[3 + cc]])
                for cc in range(2):
                    P.op("act", mk("copy", out=yfT[:, cc, j * 512:(j + 1) * 512], in_=ps[3 + cc][:, :]),
                         reads=[PS[3 + cc]], writes=[T_yf[cc][j]])
            if not last:
                for cc in range(2):
                    for tl in range(2):
                        for cs in range(2):
                            P.op("pe", mk("matmul",
                                out=ps[5][:, 0:256], lhsT=ucs[:, 16 + tl, cc, cs, :], rhs=dftC[:, cs, tl, :],
                                start=(tl == 0 and cs == 0), stop=(tl == 1 and cs == 1)),
                                reads=[T_ucs[16 + tl][cc], T_const], writes=[PS[5]])
                    P.op("act", mk("copy", out=yfT[:, cc, 2048:2304], in_=ps[5][:, 0:256]),
                         reads=[PS[5]], writes=[T_yf[cc][4]])
            if l == 0:
                tap("yf0", yfT.rearrange("p k t -> p (k t)"), [T_yf[k][b] for k in range(2) for b in range(5)], [128, 2 * NTOK], BF16)
            if stop_after == "four":
                break
            ph_reset()
            ZW = 2312
            zb, _ = ph_alloc("zb", ZW, F32)
            T_z = [Tile("z%d" % b) for b in range(5)]
            t_zpad = Tile("zpad")
            ph_tiles(T_z + [t_zpad])
            cgs, t_cgs = ph_alloc("cgs", 512, F32)
            c1, t_c1 = ph_alloc("c1", 512, F32)
            c2, t_c2 = ph_alloc("c2", 512, F32)
            defer_ada = (l + 1 < n_layers)
            if defer_ada:
                ada_slots = [ph_alloc("adc%d" % i, 1024, BF16) for i in range(8)]
            ph_commit()
            ada_pairs = ada_begin(l + 1, ada_slots) if defer_ada else []
            wcx, t_wcx = w_next("conv_cx", 0)
            wcx3 = wcx.rearrange("p (k n) -> p k n", k=8)
            wcb, t_wcb = w_next("conv_b", 0)
            wcb3 = wcb[:, 0:2048].rearrange("p (k n) -> p k n", k=8)
            zoff = [1, 513, 1025, 1537, 2051]
            for padc in (0, 2049, 2050, 2307):
                P.op("dve", mk("memset", zb[:, padc:padc + 1], 0.0), writes=[t_zpad])
            for cc in range(2):
                for b in qblocks:
                    t0, tn = TB[b]
                    for ci in range(2):
                        for kc in range(8):
                            P.op("pe", mk("matmul",
                                out=ps[ci][:, 0:tn], lhsT=wcx3[:, kc, (cc * 2 + ci) * 128:(cc * 2 + ci + 1) * 128],
                                rhs=hT[:, kc, t0:t0 + tn], start=(kc == 0), stop=(kc == 7)),
                                reads=[t_wcx, T_h[kc][b]], writes=[PS[ci]])
                    P.op("act", mk("activation",
                        out=cgs[:, 0:tn], in_=ps[0][:, 0:tn], func=AF.Identity, bias=vcol(l, 82 + cc * 2), scale=1.0),
                        reads=[PS[0], T_const], writes=[t_cgs])
                    P.op("dve", mk("scalar_tensor_tensor",
                        out=zb[:, zoff[b]:zoff[b] + tn], in0=ps[1][:, 0:tn], scalar=vcol(l, 83 + cc * 2), in1=cgs[:, 0:tn],
                        op0=ALU.add, op1=ALU.mult), reads=[PS[1], t_cgs, T_const], writes=[T_z[b]])
                for b in qblocks:
                    t0, tn = TB[b]
                    zo = zoff[b]
                    nb = [T_z[x] for x in (b - 1, b, b + 1) if 0 <= x < 4 and b < 4] if b < 4 else [T_z[4]]
                    for kc in range(8):
                        P.op("pe", mk("matmul",
                            out=ps[2][:, 0:tn], lhsT=wcb3[:, kc, cc * 128:(cc + 1) * 128], rhs=hT[:, kc, t0:t0 + tn],
                            start=(kc == 0), stop=(kc == 7)), reads=[t_wcb, T_h[kc][b]], writes=[PS[2]])
                    P.op("dve", mk("tensor_scalar",
                        out=c1[:, 0:tn], in0=zb[:, zo - 1:zo - 1 + tn], scalar1=vcol(l, 112 + cc), scalar2=vcol(l, 118 + cc),
                        op0=ALU.mult, op1=ALU.add), reads=nb + [t_zpad, T_const], writes=[t_c1])
                    P.op("dve", mk("scalar_tensor_tensor",
                        out=c2[:, 0:tn], in0=zb[:, zo:zo + tn], scalar=vcol(l, 114 + cc), in1=c1[:, 0:tn],
                        op0=ALU.mult, op1=ALU.add), reads=nb + [t_c1, T_const], writes=[t_c2])
                    P.op("dve", mk("scalar_tensor_tensor",
                        out=c1[:, 0:tn], in0=zb[:, zo + 1:zo + 1 + tn], scalar=vcol(l, 116 + cc), in1=c2[:, 0:tn],
                        op0=ALU.mult, op1=ALU.add), reads=nb + [t_zpad, t_c2, T_const], writes=[t_c1])
                    P.op("dve", mk("scalar_tensor_tensor",
                        out=ycT[:, cc, t0:t0 + tn], in0=ps[2][:, 0:tn], scalar=vcol(l, 86 + cc), in1=c1[:, 0:tn],
                        op0=ALU.add, op1=ALU.mult), reads=[PS[2], t_c1, T_const], writes=[T_yc[cc][b]])
            ada_end(l + 1, ada_pairs, 0)
            if l == 0:
                tap("yc0", ycT.rearrange("p k t -> p (k t)"), [T_yc[k][b] for k in range(2) for b in range(5)], [128, 2 * NTOK], BF16)
            if stop_after == "conv":
                break
            ph_reset()
            sig = [ph_alloc("sig%d" % i, 512, F32) for i in range(3)]
            mm = [ph_alloc("mm%d" % i, 512, F32) for i in range(3)]
            ybs = [[ph_alloc("yb%d_%d" % (p_, i), 512, BF16) for i in range(2)] for p_ in range(2)]
            if defer_ada:
                ada_slots = [ph_alloc("adm%d" % i, 1024, BF16) for i in range(7)]
            ph_commit()
            mpend = []
            gb_cnt = 0
            for g in range(4):
                ada_pairs = ada_begin(l + 1, ada_slots) if defer_ada else []
                wgfc, t_wgfc = w_next("gate_fc", g)
                wgfc3 = wgfc.rearrange("p (k n) -> p k n", k=8)
                wga, t_wga = w_next("gate_a_wo", g)
                wga3 = wga[:, 0:2048].rearrange("p (k n) -> p k n", k=8)
                wo3 = wga[:, 2048:4096].rearrange("p (k n) -> p k n", k=8)
                ww, t_ww = w_next("wout", g)
                ww3 = ww[:, 0:2048].rearrange("p (k n) -> p k n", k=2)
                for b in qblocks:
                    t0, tn = TB[b]
                    m = 1 if b == 4 else 0
                    yb = ybs[gb_cnt % 2]
                    gb_cnt += 1
                    for dd in range(2):
                        db = g * 2 + dd
                        for br in range(3):
                            for kc in range(8):
                                if br < 2:
                                    lh = wgfc3[:, kc, br * 256 + dd * 128: br * 256 + (dd + 1) * 128]
                                    tw = t_wgfc
                                else:
                                    lh = wga3[:, kc, dd * 128:(dd + 1) * 128]
                                    tw = t_wga
                                P.op("pe", mk("matmul",
                                    out=ps[br][:, 0:tn], lhsT=lh, rhs=hT[:, kc, t0:t0 + tn], start=(kc == 0), stop=(kc == 7)),
                                    reads=[tw, T_h[kc][b]], writes=[PS[br]])
                            sa, st_ = sig[br]
                            P.op("act", mk("activation",
                                out=sa[:, 0:tn], in_=ps[br][:, 0:tn], func=AF.Sigmoid, bias=vcol(l, 88 + br * 8 + db), scale=1.0),
                                reads=[PS[br], T_const], writes=[st_])
                        if dd == 1 and mpend:
                            mpend.pop()()
                        srcs = [(0, [yfT[:, 0, t0:t0 + tn], yfT[:, 1, t0:t0 + tn]], [T_yf[0][b], T_yf[1][b]]),
                                (2, [ycT[:, 0, t0:t0 + tn], ycT[:, 1, t0:t0 + tn]], [T_yc[0][b], T_yc[1][b]]),
                                (4, [oT[:, hh, t0:t0 + tn] for hh in range(4)], [T_o[hh][b] for hh in range(4)])]
                        for br, (k0, rl, tl_) in enumerate(srcs):
                            for i, (ra, rt) in enumerate(zip(rl, tl_)):
                                P.op("pe", mk("matmul",
                                    out=ps[3 + br][:, 0:tn], lhsT=wo3[:, k0 + i, dd * 128:(dd + 1) * 128], rhs=ra,
                                    start=(i == 0), stop=(i == len(rl) - 1)), reads=[t_wga, rt], writes=[PS[3 + br]])
                            sa, st_ = sig[br]
                            ma, mt = mm[br]
                            P.op("dve", mk("tensor_tensor",
                                out=ma[:, 0:tn], in0=ps[3 + br][:, 0:tn], in1=sa[:, 0:tn], op=ALU.mult),
                                reads=[PS[3 + br], st_], writes=[mt])
                        P.op("dve", mk("tensor_tensor", out=mm[0][0][:, 0:tn], in0=mm[0][0][:, 0:tn], in1=mm[1][0][:, 0:tn], op=ALU.add),
                             reads=[mm[0][1], mm[1][1]], writes=[mm[0][1]])
                        P.op("dve", mk("tensor_tensor", out=yb[dd][0][:, 0:tn], in0=mm[0][0][:, 0:tn], in1=mm[2][0][:, 0:tn], op=ALU.add),
                             reads=[mm[0][1], mm[2][1]], writes=[yb[dd][1]])
                    def wout_fn(b=b, t0=t0, tn=tn, m=m, yb=yb, ww3=ww3, t_ww=t_ww):
                        for d2 in range(8):
                            bank = 6 + (d2 % 2)
                            for dd in range(2):
                                P.op("pe", mk("matmul",
                                    out=ps[bank][:, 0:tn], lhsT=ww3[:, dd, d2 * 128:(d2 + 1) * 128], rhs=yb[dd][0][:, 0:tn],
                                    start=(dd == 0), stop=(dd == 1)), reads=[t_ww, yb[dd][1]], writes=[PS[bank]])
                            P.op("dve", mk("scalar_tensor_tensor",
                                out=xT[:, d2, t0:t0 + tn], in0=ps[bank][:, 0:tn], scalar=mod_col(l, 2, d2, m), in1=xT[:, d2, t0:t0 + tn],
                                op0=ALU.mult, op1=ALU.add), reads=[PS[bank], T_x[d2][b], T_mod[l]], writes=[T_x[d2][b]])
                    assert not mpend
                    mpend.append(wout_fn)
                ada_end(l + 1, ada_pairs, 0)
            if mpend:
                mpend.pop()()
            if l == 0:
                tap("x1", xT.rearrange("p k t -> p (k t)"), [T_x[k][b] for k in range(8) for b in range(5)], [128, 8 * NTOK], F32)
            if stop_after == "merge":
                break
            norm_phase(l, 1, qblocks, commit=False)
            sg = [ph_alloc("sg%d" % i, 512, F32) for i in range(2)]
            act = [ph_alloc("act%d" % i, 512, BF16) for i in range(8)]
            if last:
                ost = [ph_alloc("ost%d" % i, 512, F32) for i in range(2)]
            if defer_ada:
                ada_slots = [ph_alloc("adf%d" % i, 1024, BF16) for i in range(2)]
            ph_commit()
            acount = 0
            scount = 0
            fpend = []
            for fg in range(6):
                nf = 4 if fg < 5 else 2
                ada_pairs = ada_begin(l + 1, ada_slots) if defer_ada else []
                wg_, t_wg = w_next("ffn_g", fg)
                wg3 = wg_[:, 0:8 * nf * 128].rearrange("p (k n) -> p k n", k=8)
                wu_, t_wu = w_next("ffn_u", fg)
                wu3 = wu_[:, 0:8 * nf * 128].rearrange("p (k n) -> p k n", k=8)
                wd_, t_wd = w_next("ffn_d", fg)
                wd3 = wd_[:, 0:nf * 1024].rearrange("p (k n) -> p k n", k=nf)
                for b in qblocks:
                    t0, tn = TB[b]
                    m = 1 if b == 4 else 0
                    acts = []
                    for fb in range(nf):
                        bg_, bu_ = (fb % 2) * 2, (fb % 2) * 2 + 1
                        for (w3, tw, bank) in ((wg3, t_wg, bg_), (wu3, t_wu, bu_)):
                            for kc in range(8):
                                P.op("pe", mk("matmul",
                                    out=ps[bank][:, 0:tn], lhsT=w3[:, kc, fb * 128:(fb + 1) * 128], rhs=hT[:, kc, t0:t0 + tn],
                                    start=(kc == 0), stop=(kc == 7)), reads=[tw, T_h[kc][b]], writes=[PS[bank]])
                        sa, st_ = sg[scount % 2]
                        scount += 1
                        aa, at = act[acount % 8]
                        acount += 1
                        acts.append((aa, at))
                        P.op("act", mk("activation", out=sa[:, 0:tn], in_=ps[bg_][:, 0:tn], func=AF.Silu),
                             reads=[PS[bg_]], writes=[st_])
                        P.op("dve", mk("tensor_tensor",
                            out=aa[:, 0:tn], in0=ps[bu_][:, 0:tn], in1=sa[:, 0:tn], op=ALU.mult),
                            reads=[PS[bu_], st_], writes=[at])
                    def down_fn(b=b, t0=t0, tn=tn, m=m, acts=list(acts), wd3=wd3, t_wd=t_wd, nf=nf):
                        for d2 in range(8):
                            bank = 4 + (d2 % 4)
                            for fb in range(nf):
                                aa, at = acts[fb]
                                P.op("pe", mk("matmul",
                                    out=ps[bank][:, 0:tn], lhsT=wd3[:, fb, d2 * 128:(d2 + 1) * 128], rhs=aa[:, 0:tn],
                                    start=(fb == 0), stop=(fb == nf - 1)), reads=[t_wd, at], writes=[PS[bank]])
                            P.op("dve", mk("scalar_tensor_tensor",
                                out=xT[:, d2, t0:t0 + tn], in0=ps[bank][:, 0:tn], scalar=mod_col(l, 5, d2, m), in1=xT[:, d2, t0:t0 + tn],
                                op0=ALU.mult, op1=ALU.add), reads=[PS[bank], T_x[d2][b], T_mod[l]], writes=[T_x[d2][b]])
                    if fpend:
                        fpend.pop()()
                    fpend.append(down_fn)
                ada_end(l + 1, ada_pairs, 0)
            if fpend:
                fpend.pop()()
            if defer_ada:
                assert ada_state["next"] == 48, ada_state
                ada_state["next"] = 0
                adaln_finish(l + 1)
            if l == 0:
                tap("x2", xT.rearrange("p k t -> p (k t)"), [T_x[k][b] for k in range(8) for b in range(5)], [128, 8 * NTOK], F32)

        T_out = Tile("out")
        if stop_after is None:
            sq, t_sq, rstd, t_rstd = nset["sq"], nset["t_sq"], nset["rstd"], nset["t_rstd"]
            c = 0
            fgo = DEPTH * NV_L
            for b in range(4):
                t0, tn = TB[b]
                rms_rstd([T_x[k][b] for k in range(8)], lambda kc, t0=t0, tn=tn: xT[:, kc, t0:t0 + tn], 8, tn,
                         1.0 / D, sq, t_sq, rstd, t_rstd, 7)
                for kc in range(8):
                    oa, ot = ost[c % 2]
                    c += 1
                    P.op("dve", mk("scalar_tensor_tensor",
                        out=oa[:, 0:tn], in0=xT[:, kc, t0:t0 + tn], scalar=vecs[:, fgo + kc: fgo + kc + 1],
                        in1=rstd[:, 0:tn], op0=ALU.mult, op1=ALU.mult),
                        reads=[T_x[kc][b], t_rstd, T_const], writes=[ot])
                    P.dma("sp", mk("dma_start",
                        out=out_d[kc * 128:(kc + 1) * 128, t0:t0 + tn], in_=oa[:, 0:tn]),
                        reads=[ot], writes=[T_out], sem_tile=ot)
            P.final_wait("sp", [T_out] + [t for _, t in ost])
        else:
            P.op("dve", mk("memset", xT[:, 0, 0:512], 0.0), writes=[T_out])
            P.dma("sp", mk("dma_start", out=out_d[0:128, 0:512], in_=xT[:, 0, 0:512]), reads=[T_out], writes=[T_out])
            P.final_wait("sp", [T_out])
        P.final_wait("sp", list(dbg_out.values()))
        P.emit()
    return nc


def make_in_maps(inputs):
    inp = {k: np.asarray(v) for k, v in inputs.items()}
    ct = _const_tables()
    wb = np.stack([_pack_weights(inp, l) for l in range(DEPTH)], axis=0)
    wada = np.stack([
        np.stack([_tile_kxn(inp["w_ada"][l][:, g * 128:(g + 1) * 128]) for g in range(48)], axis=0)
        for l in range(DEPTH)], axis=0).astype(np.float32)
    vecs = _pack_vecs(inp)
    bvb = np.stack([np.broadcast_to(inp["b_in"][l][V_OFF:V_OFF + 512][None, :], (128, 512)) for l in range(DEPTH)],
                   axis=0).astype(np.float32)
    bvb = np.ascontiguousarray(bvb)
    rope = np.ascontiguousarray(ct["rope"].reshape(128, 2 * L))
    cctx = _col(inp["c_ctx"])
    maps = []
    for b in range(8):
        xt = np.ascontiguousarray(np.concatenate([inp["x"][b].T, inp["ctx"][b].T], axis=1), dtype=np.float32)
        cv = np.ascontiguousarray(np.stack([_col(inp["c"][b]), cctx], axis=2).reshape(128, 16), dtype=np.float32)
        maps.append({"xt": xt, "cv": cv, "wada": wada, "wb": wb, "vecs": vecs, "bvb": bvb, "rope": rope,
                     "dftL": ct["dftL"], "dftC": ct["dftC"], "dftch": ct["dftch"], "permm": ct["permm"]})
    return maps


_NC_CACHE = {}


def kernel(**inputs):
    maps = make_in_maps(inputs)
    if "nc" not in _NC_CACHE:
        _NC_CACHE["nc"] = build_program()
    res = run_bass_kernel_spmd(_NC_CACHE["nc"], maps, core_ids=list(range(8)))
    out = np.stack([np.asarray(r["outT"]).T for r in res.results], axis=0)
    return np.ascontiguousarray(out, dtype=np.float32)
```
